# Optimizing a Trainium2 kernel written in Bass

```python
import jax, jax.numpy as jnp
from jax import lax
import numpy as np

D_MODEL = 1024
BATCH = 16
SEQ = 2048
DEPTH = 1

CTX_LEN = 256
GRID_W = 64
CHUNK = 128
ROWS_PER_CHUNK = CHUNK // GRID_W
GM_HEADS = 8
GM_HEAD_DIM = 128
GM_WIDTH = GM_HEADS * GM_HEAD_DIM
SSD_HEADS = 16
SSD_HEAD_DIM = 64
SSD_WIDTH = SSD_HEADS * SSD_HEAD_DIM
SSD_STATE = 128
SSD_GROUPS = 2
SSD_HPG = SSD_HEADS // SSD_GROUPS
SSD_CONV = 5
SSD_CHUNK = 128
CONV_CH = SSD_WIDTH + 2 * SSD_GROUPS * SSD_STATE
SSD_IN_WIDTH = SSD_WIDTH + CONV_CH + 2 * SSD_HEADS
MIX_WIDTH = GM_WIDTH + SSD_WIDTH
IN_WIDTH = 2 * GM_WIDTH + SSD_IN_WIDTH
FFN_HIDDEN = -(-8 * D_MODEL // (3 * 256)) * 256
N_MOD = 6
EPS = 1e-6

kernel_name = "hybrid_gmlp_ssd_dit_block"


def _rms(x):
    x32 = x.astype(jnp.float32)
    y = x32 * lax.rsqrt(jnp.mean(x32 * x32, axis=-1, keepdims=True) + EPS)
    return y.astype(x.dtype)


def _conv_centred(x, w, b):
    ch = x.shape[-1]
    y = lax.conv_general_dilated(
        x, w[:, None, :], window_strides=(1,),
        padding=[(SSD_CONV // 2, SSD_CONV // 2)],
        dimension_numbers=("NWC", "WIO", "NWC"), feature_group_count=ch)
    return y + b


def _ssd_chunked(x, dt, A, bm, cm, h0):
    b, l, _, _ = x.shape
    nc = l // SSD_CHUNK
    f32 = jnp.float32
    x = x.astype(f32).reshape(b, nc, SSD_CHUNK, SSD_GROUPS, SSD_HPG, SSD_HEAD_DIM)
    dt = dt.astype(f32).reshape(b, nc, SSD_CHUNK, SSD_GROUPS, SSD_HPG)
    bm = bm.astype(f32).reshape(b, nc, SSD_CHUNK, SSD_GROUPS, SSD_STATE)
    cm = cm.astype(f32).reshape(b, nc, SSD_CHUNK, SSD_GROUPS, SSD_STATE)
    a = dt * A.astype(f32).reshape(SSD_GROUPS, SSD_HPG)
    a_cs = jnp.cumsum(a, axis=2)
    xdt = x * dt[..., None]
    seg = a_cs[:, :, :, None] - a_cs[:, :, None, :]
    tri = jnp.tril(jnp.ones((SSD_CHUNK, SSD_CHUNK), dtype=bool))[:, :, None, None]
    decay = jnp.exp(jnp.where(tri, seg, -jnp.inf))
    cb = jnp.einsum('bctgn,bcsgn->bctsg', cm, bm)
    y_diag = jnp.einsum('bctsg,bctsge,bcsgep->bctgep', cb, decay, xdt)
    decay_to_end = jnp.exp(a_cs[:, :, -1:] - a_cs)
    states = jnp.einsum('bcsgn,bcsge,bcsgep->bcgepn', bm, decay_to_end, xdt)
    chunk_decay = jnp.exp(a_cs[:, :, -1])

    def step(h, inp):
        s, d = inp
        return d[..., None, None] * h + s, h

    h_final, h_prev = lax.scan(step, h0.astype(f32),
                               (jnp.moveaxis(states, 1, 0), jnp.moveaxis(chunk_decay, 1, 0)))
    h_prev = jnp.moveaxis(h_prev, 0, 1)
    y_off = jnp.einsum('bctgn,bcgepn,bctge->bctgep', cm, h_prev, jnp.exp(a_cs))
    y = (y_diag + y_off).reshape(b, l, SSD_HEADS, SSD_HEAD_DIM)
    return y, h_final


def _ssd_bidirectional(xs, bm, cm, dt_raw, a_log, dt_bias, st_f, st_b):
    f32 = jnp.float32
    dt_f = jax.nn.softplus((dt_raw[..., :SSD_HEADS] + dt_bias[0]).astype(f32))
    y_f, fin_f = _ssd_chunked(xs, dt_f, -jnp.exp(a_log[0].astype(f32)), bm, cm, st_f)
    dt_b = jax.nn.softplus((dt_raw[..., SSD_HEADS:] + dt_bias[1]).astype(f32))
    y_b, fin_b = _ssd_chunked(jnp.flip(xs, 1), jnp.flip(dt_b, 1),
                              -jnp.exp(a_log[1].astype(f32)),
                              jnp.flip(bm, 1), jnp.flip(cm, 1), st_b)
    return y_f + jnp.flip(y_b, 1), fin_f, fin_b


def _token_mixer(h, st_f, st_b, n_chunks, with_output, w_in, gm_norm_w, gm_ws, gm_bs,
                 conv_w, conv_b, a_log, dt_bias, d_skip, ssd_norm_w, w_out):
    b, l, _ = h.shape
    proj = h @ (w_in if with_output else w_in[:, 2 * GM_WIDTH:])
    ssd_in = proj[..., proj.shape[-1] - SSD_IN_WIDTH:]
    z, xbc, dt_raw = jnp.split(ssd_in, [SSD_WIDTH, SSD_WIDTH + CONV_CH], axis=-1)
    xbc = jax.nn.silu(_conv_centred(xbc, conv_w, conv_b))
    xs, bm, cm = jnp.split(xbc, [SSD_WIDTH, SSD_WIDTH + SSD_GROUPS * SSD_STATE], axis=-1)
    xs = xs.reshape(b, l, SSD_HEADS, SSD_HEAD_DIM)
    bm = bm.reshape(b, l, SSD_GROUPS, SSD_STATE)
    cm = cm.reshape(b, l, SSD_GROUPS, SSD_STATE)
    y, fin_f, fin_b = _ssd_bidirectional(xs, bm, cm, dt_raw, a_log, dt_bias, st_f, st_b)
    if not with_output:
        return None, fin_f, fin_b
    y = y.astype(h.dtype) + xs * d_skip[:, None]
    g = (y.reshape(b, l, SSD_WIDTH) * jax.nn.silu(z)).reshape(b, l, SSD_GROUPS, SSD_WIDTH // SSD_GROUPS)
    ssd_out = _rms(g).reshape(b, l, SSD_WIDTH) * ssd_norm_w
    u, v = jnp.split(jax.nn.gelu(proj[..., :2 * GM_WIDTH]), 2, axis=-1)
    v = _rms(v.reshape(b, l, GM_HEADS, GM_HEAD_DIM)) * gm_norm_w.reshape(GM_HEADS, GM_HEAD_DIM)
    v = v.reshape(b, n_chunks, CHUNK, GM_HEADS, GM_HEAD_DIM)
    mixed = jnp.einsum('hts,bcshd->bcthd', gm_ws, v) + gm_bs.T[:, :, None]
    gm_out = u * mixed.reshape(b, l, GM_WIDTH)
    out = jnp.concatenate([gm_out, ssd_out], axis=-1) @ w_out
    return out, fin_f, fin_b


def _swiglu(h, w13, w2):
    gate, up = jnp.split(h @ w13, 2, axis=-1)
    return (jax.nn.silu(gate) * up) @ w2


def setup_inputs(seed: int = 0) -> dict:
    key = jax.random.key(seed)
    ks = jax.random.split(key, 24)
    f32 = jnp.float32

    def nrm(k, shape, scale):
        return jax.random.normal(k, shape, f32) * scale

    dt0 = jnp.exp(jax.random.uniform(ks[13], (DEPTH, 2, SSD_HEADS), f32,
                                     np.log(1e-3), np.log(1e-1)))
    return {
        "x": nrm(ks[0], (BATCH, SEQ, D_MODEL), 1.0),
        "c": nrm(ks[1], (BATCH, D_MODEL), 1.0),
        "ctx": nrm(ks[2], (BATCH, CTX_LEN, D_MODEL), 1.0),
        "c_ctx": nrm(ks[3], (D_MODEL,), 1.0),
        "w_ada": nrm(ks[4], (DEPTH, D_MODEL, N_MOD * D_MODEL), 0.5 * D_MODEL ** -0.5),
        "b_ada": nrm(ks[5], (DEPTH, N_MOD * D_MODEL), 0.02),
        "norm1_w": 1.0 + nrm(ks[6], (DEPTH, D_MODEL), 0.02),
        "w_in": nrm(ks[7], (DEPTH, D_MODEL, IN_WIDTH), D_MODEL ** -0.5),
        "gm_norm_w": 1.0 + nrm(ks[8], (DEPTH, GM_WIDTH), 0.02),
        "gm_ws": nrm(ks[9], (DEPTH, GM_HEADS, CHUNK, CHUNK), CHUNK ** -0.5),
        "gm_bs": 1.0 + nrm(ks[10], (DEPTH, GM_HEADS, CHUNK), 0.02),
        "conv_w": nrm(ks[11], (DEPTH, SSD_CONV, CONV_CH), SSD_CONV ** -0.5),
        "conv_b": nrm(ks[12], (DEPTH, CONV_CH), 0.02),
        "ssd_A_log": jnp.log(jax.random.uniform(ks[14], (DEPTH, 2, SSD_HEADS), f32, 1.0, 16.0)),
        "ssd_dt_bias": dt0 + jnp.log(-jnp.expm1(-dt0)),
        "ssd_D": 1.0 + nrm(ks[15], (DEPTH, SSD_HEADS), 0.02),
        "ssd_norm_w": 1.0 + nrm(ks[16], (DEPTH, SSD_WIDTH), 0.02),
        "w_out": nrm(ks[17], (DEPTH, MIX_WIDTH, D_MODEL), MIX_WIDTH ** -0.5),
        "norm2_w": 1.0 + nrm(ks[18], (DEPTH, D_MODEL), 0.02),
        "ffn_w13": nrm(ks[19], (DEPTH, D_MODEL, 2 * FFN_HIDDEN), D_MODEL ** -0.5),
        "ffn_w2": nrm(ks[20], (DEPTH, FFN_HIDDEN, D_MODEL), FFN_HIDDEN ** -0.5),
        "final_norm_w": 1.0 + nrm(ks[21], (D_MODEL,), 0.02),
    }


def reference(x, c, ctx, c_ctx, w_ada, b_ada, norm1_w, w_in, gm_norm_w, gm_ws, gm_bs,
              conv_w, conv_b, ssd_A_log, ssd_dt_bias, ssd_D, ssd_norm_w, w_out,
              norm2_w, ffn_w13, ffn_w2, final_norm_w):
    rows = x.shape[1] // GRID_W
    lat_chunks = rows // ROWS_PER_CHUNK
    ctx_chunks = ctx.shape[1] // CHUNK
    n_b = ctx.shape[0]
    zero_state = jnp.zeros((n_b, SSD_GROUPS, SSD_HPG, SSD_HEAD_DIM, SSD_STATE), jnp.float32)
    for i in range(DEPTH):
        last = i == DEPTH - 1
        m_lat = (jax.nn.silu(c) @ w_ada[i] + b_ada[i])[:, None, :]
        m_ctx = jax.nn.silu(c_ctx) @ w_ada[i] + b_ada[i]
        sh1, sc1, g1, sh2, sc2, g2 = jnp.split(m_lat, N_MOD, axis=-1)
        csh1, csc1, cg1, csh2, csc2, cg2 = jnp.split(m_ctx, N_MOD, axis=-1)
        mix_params = (w_in[i], gm_norm_w[i], gm_ws[i], gm_bs[i], conv_w[i], conv_b[i],
                      ssd_A_log[i], ssd_dt_bias[i], ssd_D[i], ssd_norm_w[i], w_out[i])
        hc = _rms(ctx) * norm1_w[i] * (1.0 + csc1) + csh1
        ctx_mix, st_f, st_b = _token_mixer(hc, zero_state, zero_state, ctx_chunks,
                                           not last, *mix_params)
        hl = _rms(x) * norm1_w[i] * (1.0 + sc1) + sh1
        lat_mix, _, _ = _token_mixer(hl, st_f, st_b, lat_chunks, True, *mix_params)
        x = x + g1 * lat_mix
        hl = _rms(x) * norm2_w[i] * (1.0 + sc2) + sh2
        x = x + g2 * _swiglu(hl, ffn_w13[i], ffn_w2[i])
        if not last:
            ctx = ctx + cg1 * ctx_mix
            hc = _rms(ctx) * norm2_w[i] * (1.0 + csc2) + csh2
            ctx = ctx + cg2 * _swiglu(hc, ffn_w13[i], ffn_w2[i])
    return _rms(x) * final_norm_w
```

```python
from concourse.bass_utils import run_bass_kernel_spmd
import numpy as np
import concourse.bass as bass
import concourse.mybir as mybir

F32 = mybir.dt.float32
BF16 = mybir.dt.bfloat16
AF = mybir.ActivationFunctionType
ALU = mybir.AluOpType
AX = mybir.AxisListType

PE, ACT, DVE, POOL, SP = "pe", "act", "dve", "pool", "sp"
ENGS = [PE, ACT, DVE, POOL, SP]
N_DMA_SEMS = 8


class Op:
    __slots__ = ("eng", "fn", "reads", "writes", "dma", "idx", "deps", "signal",
                 "sigval", "dsem", "dval", "guard")

    def __init__(self, eng, fn, reads, writes, dma):
        self.eng, self.fn, self.reads, self.writes, self.dma = eng, fn, reads, writes, dma
        self.deps = []
        self.signal = False
        self.sigval = 0
        self.dsem = None
        self.dval = 0
        self.guard = None


class Prog:
    def __init__(self, nc):
        self.nc = nc
        self.ops = []
        self.last_w = {}
        self.readers = {}
        self.dma_rr = {e: 0 for e in ENGS}
        self.dma_last = {}
        self.last_eng = {}
        self.bar = {}

    def add(self, eng, fn, reads=(), writes=(), dma=False):
        reads = list(reads)
        writes = list(writes)
        ps_r = [k for k in reads if isinstance(k, str) and k.startswith("ps")]
        if ps_r:
            reads = [k for k in reads if k not in ps_r]
            writes = writes + [k for k in ps_r if k not in writes]
        op = Op(eng, fn, reads, writes, dma)
        op.idx = len(self.ops)
        deps = {}

        def dep(p, kind):
            if p is None:
                return
            if p.dma or dma:
                deps[p.idx] = p
                return
            if p.eng == eng:
                if eng == PE:
                    return
                deps[p.idx] = p
                return
            deps[p.idx] = p

        for k in reads:
            dep(self.last_w.get(k), "raw")
        for k in writes:
            dep(self.last_w.get(k), "waw")
            for r in self.readers.get(k, ()):
                dep(r, "war")
        for k in reads:
            lst = self.readers.setdefault(k, [])
            if not dma and eng != POOL:
                lst[:] = [r for r in lst if r.dma or r.eng != eng]
            lst.append(op)
        for k in writes:
            self.last_w[k] = op
            self.readers[k] = []
        for p in self.bar.pop(eng, ()):
            if p.dma or dma or p.eng != eng:
                deps[p.idx] = p
        op.deps = list(deps.values())
        if not dma:
            self.last_eng[eng] = op
        if dma:
            slot = self.dma_rr[eng] % N_DMA_SEMS
            self.dma_rr[eng] += 1
            op.dsem = (eng, slot)
            prev = self.dma_last.get(op.dsem)
            op.dval = (prev.dval if prev else 0) + 16
            op.guard = prev
            self.dma_last[op.dsem] = op
        self.ops.append(op)
        return op

    def barrier(self):
        pend = list(self.last_eng.values()) + list(self.dma_last.values())
        for e in ENGS:
            self.bar[e] = list(pend)

    def emit(self, sems, dsems):
        nc = self.nc
        for op in self.ops:
            for p in op.deps:
                if not p.dma:
                    p.signal = True
        cnt = {e: 0 for e in ENGS}
        for op in self.ops:
            if op.signal and not op.dma:
                cnt[op.eng] += 1
                op.sigval = cnt[op.eng]
        per = {e: [o for o in self.ops if o.eng == e] for e in ENGS}
        handles = {PE: nc.tensor, ACT: nc.scalar, DVE: nc.vector, POOL: nc.gpsimd, SP: nc.sync}
        self.n_waits = 0

        def run(eng, h=None):
            if h is None:
                h = handles[eng]
            waited = {}
            for op in per[eng]:
                need = {}
                for p in op.deps:
                    if p.dma:
                        key, val = ("d",) + p.dsem, p.dval
                    else:
                        key, val = ("e", p.eng), p.sigval
                    if need.get(key, 0) < val:
                        need[key] = val
                if op.guard is not None:
                    key = ("d",) + op.dsem
                    if need.get(key, 0) < op.guard.dval:
                        need[key] = op.guard.dval
                todo = []
                for key, val in need.items():
                    if waited.get(key, 0) >= val:
                        continue
                    waited[key] = val
                    todo.append((dsems[key[1:]] if key[0] == "d" else sems[key[1]], val))
                for s, val in todo[:-1]:
                    h.wait_ge(s, val)
                    self.n_waits += 1
                ins = op.fn(h)
                if todo:
                    ins._wait_ge(todo[-1][0], todo[-1][1])
                if op.dma:
                    ins.then_inc(dsems[op.dsem], 16)
                elif op.signal:
                    ins.then_inc(sems[eng], 1)
            for (e, slot), last in self.dma_last.items():
                if e == eng:
                    h.wait_ge(dsems[(e, slot)], last.dval)

        return run


def run_prog(nc, prog):
    from contextlib import ExitStack
    with ExitStack() as st:
        sems = {e: st.enter_context(nc.semaphore("s_" + e)) for e in ENGS}
        dsems = {}
        for e in ENGS:
            if prog.dma_rr[e] > 0:
                for i in range(N_DMA_SEMS):
                    dsems[(e, i)] = st.enter_context(nc.semaphore("d_%s%d" % (e, i)))
        run = prog.emit(sems, dsems)
        block = st.enter_context(nc.Block())

        @block.tensor
        def _(e):
            run(PE, e)

        @block.scalar
        def _(e):
            run(ACT, e)

        @block.vector
        def _(e):
            run(DVE, e)

        @block.gpsimd
        def _(e):
            run(POOL, e)

        @block.sync
        def _(e):
            run(SP, e)
D = 1024
EPS = 1e-6
U0, V0, Z0, X0, DT0 = 0, 1024, 2048, 3072, 4608
NA = 1568


def bcl(a, n):
    return bass.AP(a.tensor, a.offset, list(a.ap) + [[0, n]])


def bcm(a, n):
    return bass.AP(a.tensor, a.offset, [a.ap[0], [0, n]] + list(a.ap[1:]))


def prow(a, n=128):
    return bass.AP(a.tensor, a.offset, [[0, n]] + list(a.ap[1:]))


def build(T, TC, NB):
    nc = bass.Bass("TRN2", target_bir_lowering=False)
    NT, NTC = T // 128, TC // 128
    NJ = NB + 1

    def din(name, shape, dt=F32):
        return nc.dram_tensor(name, shape, dt, kind="ExternalInput").ap()

    x = din("x", [NB, T, D]); ctx = din("ctx", [NB, TC, D]); cT = din("cT", [128, 8, NJ])
    w_ada = din("w_ada", [D, 6 * D]); b_ada = din("b_ada", [1, 6 * D])
    w_in = din("w_in", [D, 4640]); w_out = din("w_out", [2 * D, D])
    w13 = din("w13", [D, 5632]); w2 = din("w2", [2816, D])
    n1w = din("n1w", [128, 8]); n2w = din("n2w", [128, 8]); snw = din("snw", [128, 8])
    gnw = din("gnw", [1, D]); bsv = din("bsv", [1, D]); fnw = din("fnw", [1, D])
    wsT = din("wsT", [128, 8, 128]); cw = din("cw", [128, 12, 5]); cbv = din("cbv", [128, 12])
    alog = din("alog", [1, 32]); dtbias = din("dtbias", [1, 32]); dD = din("dD", [1, 16])
    out = nc.dram_tensor("out", [NB, T, D], F32, kind="ExternalOutput").ap()
    m_s = nc.dram_tensor("m_s", [NJ, 6 * D], F32).ap()
    z_s = nc.dram_tensor("z_s", [T, D], BF16).ap()
    gm_s = nc.dram_tensor("gm_s", [T // 128, 128, D], BF16).ap()
    x1_s = nc.dram_tensor("x1_s", [T, D], F32).ap()

    P = Prog(nc)
    cnt = [0]

    def sbt(name, shape, dt, off):
        n = 1
        for s in shape[1:]:
            n *= s
        nb = n * (4 if dt == F32 else 2)
        cnt[0] += 1
        t = nc.alloc_sbuf_tensor_at("%s_%d" % (name, cnt[0]), shape, dt, offset=off + 16512)
        return t, off + ((nb + 31) // 32) * 32

    class Alloc:
        def __init__(self, off):
            self.off = off

        def __call__(self, name, shape, dt=F32):
            t, self.off = sbt(name, shape, dt, self.off)
            assert self.off <= 212800, (name, self.off)
            return t

    def mm(o, l, r, st, sp, R, W):
        P.add(PE, lambda h: h.matmul(o, l, r, start=st, stop=sp), R, W)

    def tr(o, i, R, W):
        P.add(PE, lambda h: h.transpose(o, i, ident[:]), list(R) + ["ident"], W)

    def act(o, i, f, R, W, bias=None, scale=None, accum=None):
        kw = {}
        if bias is not None:
            kw["bias"] = bias
        if scale is not None:
            kw["scale"] = scale
        if accum is not None:
            kw["accum_out"] = accum
        P.add(ACT, lambda h: h.activation(out=o, in_=i, func=f, **kw), R, W)

    def tt(e, o, a, b, op, R, W):
        P.add(e, lambda h: h.tensor_tensor(out=o, in0=a, in1=b, op=op), R, W)

    def ts(e, o, a, s1, s2, op0, op1, R, W):
        if s2 is None:
            P.add(e, lambda h: h.tensor_scalar(out=o, in0=a, scalar1=s1, scalar2=None, op0=op0), R, W)
        else:
            P.add(e, lambda h: h.tensor_scalar(out=o, in0=a, scalar1=s1, scalar2=s2, op0=op0, op1=op1), R, W)

    def stt(e, o, a, s, b, op0, op1, R, W, accum=None):
        if accum is None:
            P.add(e, lambda h: h.scalar_tensor_tensor(out=o, in0=a, scalar=s, in1=b, op0=op0, op1=op1), R, W)
        else:
            P.add(e, lambda h: h.scalar_tensor_tensor(out=o, in0=a, scalar=s, in1=b, op0=op0, op1=op1, accum_out=accum), R, W)

    def cp(e, o, i, R, W):
        if e == ACT:
            P.add(ACT, lambda h: h.copy(out=o, in_=i), R, W)
        else:
            P.add(e, lambda h: h.tensor_copy(out=o, in_=i), R, W)

    def ms(e, o, v, W):
        P.add(e, lambda h: h.memset(o, v), [], W)

    def dma(q, o, i, R, W, slow=False):
        if slow:
            P.add(q, lambda h: h.dma_start(out=o, in_=i, allow_slow_non_contiguous=True), R, W, dma=True)
        else:
            P.add(q, lambda h: h.dma_start(out=o, in_=i), R, W, dma=True)

    def asel(o, pat, cm, op, W):
        P.add(POOL, lambda h: h.affine_select(out=o, in_=o, pattern=pat, compare_op=op, fill=0.0,
                                              base=0, channel_multiplier=cm), W, W)

    pd = [nc.alloc_psum_tensor("pd%d" % i, [128, 1024], F32) for i in range(4)]

    def bank(i):
        return pd[i // 2][:, (i % 2) * 512:(i % 2 + 1) * 512]

    def bankb(i):
        return bank(i).bitcast(BF16)

    def pk(i):
        return "ps%d" % i

    CA = Alloc(0)
    ident = CA("ident", [128, 128], BF16); identf = CA("identf", [128, 128])
    Umat = CA("Umat", [128, 128]); Lmat = CA("Lmat", [128, 128]); ones = CA("ones", [128, 128])
    SLf = CA("SLf", [128, 128], BF16); SLb = CA("SLb", [128, 128], BF16)
    Ubf = CA("Ubf", [128, 128], BF16); Lbf = CA("Lbf", [128, 128], BF16)
    mLs = CA("mLs", [128, 128]); mtmp = CA("mtmp", [128, 128])
    WsT = CA("WsT", [128, 8, 128], BF16)
    cw_t = CA("cw", [128, 12, 5]); cb_t = CA("cb", [128, 12])
    n1w_t = CA("n1w", [128, 8]); n2w_t = CA("n2w", [128, 8]); snw_t = CA("snw", [128, 8])
    expA = CA("expA", [128, 32]); dtb_t = CA("dtb", [128, 32]); dD_t = CA("dD", [128, 16])
    eps1 = CA("eps1", [128, 1]); eps4 = CA("eps4", [128, 1])
    sc_t = CA("sc", [128, 8, NJ])
    modT = [CA("modT%d" % j, [128, 48]) for j in range(NJ)]
    scale1 = [CA("scale1_%d" % j, [128, 8]) for j in range(NJ)]
    scale2 = [CA("scale2_%d" % j, [128, 8]) for j in range(NJ)]
    CEND = CA.off
    assert CEND <= 9216, CEND
    RA, RB, ST, WB = 9216, 58496, 91264, 105600

    def mask(dst_f32, pat, cm, op, key):
        ms(POOL, dst_f32[:], 1.0, [key])
        asel(dst_f32[:], pat, cm, op, [key])
    mask(identf, [[1, 128]], -1, ALU.is_equal, "identf")
    mask(Umat, [[1, 128]], -1, ALU.is_ge, "Umat")
    mask(Lmat, [[-1, 128]], 1, ALU.is_ge, "Lmat")
    mask(mLs, [[-1, 128]], 1, ALU.is_gt, "mLs")
    mask(mtmp, [[1, 128]], -1, ALU.is_gt, "mtmp")
    ms(POOL, ones[:], 1.0, ["ones"])
    cp(DVE, ident[:], identf[:], ["identf"], ["ident"])
    cp(DVE, Ubf[:], Umat[:], ["Umat"], ["Ubf"])
    cp(DVE, Lbf[:], Lmat[:], ["Lmat"], ["Lbf"])
    cp(DVE, SLf[:], mLs[:], ["mLs"], ["SLf"])
    cp(DVE, SLb[:], mtmp[:], ["mtmp"], ["SLb"])
    ms(POOL, eps1[:], EPS, ["eps1"]); ms(POOL, eps4[:], 4 * EPS, ["eps4"])
    dma(POOL, WsT[:], wsT, [], ["WsT"])
    dma(SP, cw_t[:], cw, [], ["cw"]); dma(SP, cb_t[:], cbv, [], ["cb"])
    dma(SP, n1w_t[:], n1w, [], ["n1w"]); dma(SP, n2w_t[:], n2w, [], ["n2w"]); dma(SP, snw_t[:], snw, [], ["snw"])
    dma(SP, expA[:], prow(alog), [], ["expA"]); dma(SP, dtb_t[:], prow(dtbias), [], ["dtb"])
    dma(SP, dD_t[:], prow(dD), [], ["dD"])
    act(expA[:], expA[:], AF.Exp, ["expA"], ["expA"])
    dma(SP, sc_t[:], cT, [], ["sc"])
    act(sc_t[:], sc_t[:], AF.Silu, ["sc"], ["sc"])

    wA, _ = sbt("wA", [128, 8, NA], BF16, WB)
    dma(POOL, wA[:], w_in.rearrange("(k p) n -> p k n", p=128)[:, :, X0:X0 + NA], [], ["wA"])
    A0 = Alloc(RA)
    wblk = [A0("wblk%d" % i, [128, 3072]) for i in range(2)]
    bblk = A0("bblk", [NJ, 3072])
    mblk = A0("mblk", [NJ, 3072])
    it = 0
    for half in range(2):
        c0 = half * 3072
        dma(SP, bblk[:], prow(b_ada[:, c0:c0 + 3072], NJ), [], ["bblk"])
        for k in range(8):
            i = it % 2; it += 1
            dma(SP, wblk[i][:], w_ada[k * 128:(k + 1) * 128, c0:c0 + 3072], [], [("wblk", i)])
            for cb in range(6):
                mm(bank(cb)[0:NJ, :], sc_t[:, k, :], wblk[i][:, cb * 512:(cb + 1) * 512], k == 0, k == 7, ["sc", ("wblk", i)], [pk(cb)])
        for cb in range(6):
            tt(DVE, mblk[:, cb * 512:(cb + 1) * 512], bank(cb)[0:NJ, :], bblk[:, cb * 512:(cb + 1) * 512], ALU.add,
               [pk(cb), "bblk"], ["mblk"])
        dma(SP, m_s[:, c0:c0 + 3072], mblk[:], ["mblk"], ["m_s"])
    for j in range(NJ):
        dma(SP, modT[j][:], m_s[j:j + 1, :].rearrange("o (c p) -> p (o c)", p=128), ["m_s"], [("modT", j)], slow=True)
        stt(DVE, scale1[j][:], modT[j][:, 8:16], 1.0, n1w_t[:], ALU.add, ALU.mult, [("modT", j), "n1w"], [("scale1", j)])
        stt(DVE, scale2[j][:], modT[j][:, 32:40], 1.0, n2w_t[:], ALU.add, ALU.mult, [("modT", j), "n2w"], [("scale2", j)])
    P.barrier()

    def norm_to_T(src, L, scale_ap, shift_ap, mkeys, dst, dkey, xseg, xn, junk, sss, keep_key=None):
        xsegs = xseg if isinstance(xseg, list) else [xseg]
        xns = xn if isinstance(xn, list) else [xn]
        for s0 in range(0, L, 512):
            ln = min(512, L - s0)
            nt = ln // 128
            sp_ = (s0 // 512) % len(xsegs)
            xseg = xsegs[sp_]
            xn = xns[sp_ % len(xns)]
            xp_ = sp_ % len(xns)
            ss = sss[sp_] if isinstance(sss, list) else sss
            ms(DVE, ss[:, 0:8], 0.0, [("ss", sp_)])
            for i in range(nt):
                dma(SP, xseg[:, i, :], src[s0 + i * 128:s0 + (i + 1) * 128, :], [], [("xseg", sp_, i)])
                act(junk[:], xseg[:, i, :], AF.Square, [("xseg", sp_, i)], ["junk", ("ss", sp_)], accum=ss[:, i:i + 1])
            act(ss[:, 4:8], ss[:, 0:4], AF.Ln, [("ss", sp_), "eps1"], [("ss", sp_)], bias=eps1[:], scale=1.0 / D)
            act(ss[:, 8:12], ss[:, 4:8], AF.Exp, [("ss", sp_)], [("rstd", sp_)], scale=-0.5)
            for i in range(nt):
                if i % 2 == 0:
                    act(xn[:, i, :], xseg[:, i, :], AF.Identity, [("xseg", sp_, i), ("rstd", sp_)], [("xn", xp_, i)], scale=ss[:, 8 + i:9 + i])
                else:
                    ts(DVE, xn[:, i, :], xseg[:, i, :], ss[:, 8 + i:9 + i], None, ALU.mult, None,
                       [("xseg", sp_, i), ("rstd", sp_)], [("xn", xp_, i)])
            for k in range(8):
                pv = bankb(k // 2)[:, (k % 2) * 512:(k % 2) * 512 + ln]
                for i in range(nt):
                    tr(pv[:, i * 128:(i + 1) * 128], xn[:, i, k * 128:(k + 1) * 128], [("xn", xp_, i)], [pk(k // 2)])
                if k % 2 == 0:
                    act(dst[:, k, s0:s0 + ln], pv, AF.Identity, [pk(k // 2)] + mkeys, [dkey],
                        bias=shift_ap[:, k:k + 1], scale=scale_ap[:, k:k + 1])
                else:
                    ts(DVE, dst[:, k, s0:s0 + ln], pv, scale_ap[:, k:k + 1], shift_ap[:, k:k + 1], ALU.mult, ALU.add,
                       [pk(k // 2)] + mkeys, [dkey])

    def sweepA(hlT, L, wA, xbcT, dtraw):
        rr = 0
        for s0 in range(0, L, 512):
            ln = min(512, L - s0)
            for j in range(12):
                b = 4 + (rr % 4); rr += 1
                for k in range(8):
                    mm(bank(b)[:, 0:ln], wA[:, k, j * 128:(j + 1) * 128], hlT[:, k, s0:s0 + ln], k == 0, k == 7,
                       ["wA", "hlT"], [pk(b)])
                cp(ACT if j % 2 == 0 else DVE, xbcT[:, j, 2 + s0:2 + s0 + ln], bank(b)[:, 0:ln], [pk(b)], [("xbc", j)])
            for i in range(ln // 128):
                b = 4 + (rr % 4); rr += 1
                t0 = s0 + i * 128
                for k in range(8):
                    mm(bank(b)[:, 0:32], hlT[:, k, t0:t0 + 128], wA[:, k, 1536:1568], k == 0, k == 7, ["wA", "hlT"], [pk(b)])
                cp(DVE, dtraw[:, t0 // 128, :], bank(b)[:, 0:32], [pk(b)], ["dtraw"])

    def conv_stage(L, xbcT, dg, xsh, xs_tok, B_tok):
        nt = L // 128
        for j in range(12):
            for k in range(5):
                ts(DVE, dg[:, (j * 5 + k) * 128:(j * 5 + k + 1) * 128], ident[:], cw_t[:, j, k:k + 1], None, ALU.mult, None,
                   ["ident", "cw"], [("dg", j)])
            cp(POOL if j % 2 else DVE, xsh[:, j, 0:L + 2], xbcT[:, j, 1:L + 3], [("xbc", j)], [("xsh", j)])
        for j in range(12):
            segs = [(s0, min(512, L - s0)) for s0 in range(0, L, 512)]
            for si, (s0, ln) in enumerate(segs):
                bk = (j % 2) * 4 + si
                for k in range(5):
                    src = xbcT[:, j, s0 + k:s0 + k + ln] if k % 2 == 0 else xsh[:, j, s0 + k - 1:s0 + k - 1 + ln]
                    mm(bank(bk)[:, 0:ln], dg[:, (j * 5 + k) * 128:(j * 5 + k + 1) * 128], src,
                       k == 0, k == 4, [("dg", j), ("xbc", j), ("xsh", j)], [pk(bk)])
            for si, (s0, ln) in enumerate(segs):
                bk = (j % 2) * 4 + si
                act(xbcT[:, j, 2 + s0:2 + s0 + ln], bank(bk)[:, 0:ln], AF.Silu, [pk(bk), "cb"], [("xbc", j)], bias=cb_t[:, j:j + 1])
        for i in range(nt):
            pv = bankb(i % 2)
            for j in range(8):
                tr(pv[:, j * 128:(j + 1) * 128], xbcT[:, j, 2 + i * 128:2 + (i + 1) * 128], [("xbc", j)], [pk(i % 2)])
            cp(ACT if i % 2 == 0 else DVE, xs_tok[:, i, :], pv, [pk(i % 2)], [("xs", i)])
        for i0 in range(0, nt, 4):
            n = min(4, nt - i0)
            pv = bankb(2 + (i0 // 4) % 2)
            for ii in range(n):
                for g in range(2):
                    tr(pv[:, ii * 256 + g * 128:ii * 256 + (g + 1) * 128],
                       xbcT[:, 8 + g, 2 + (i0 + ii) * 128:2 + (i0 + ii + 1) * 128], [("xbc", 8 + g)], [pk(2 + (i0 // 4) % 2)])
            cp(DVE, B_tok[:, i0:i0 + n, :], pv[:, 0:n * 256].rearrange("p (a b) -> p a b", b=256),
               [pk(2 + (i0 // 4) % 2)], ["Btok"])

    def dt_stage(L, dtraw, dta):
        nt = L // 128
        dt, lndt, aa, csa, tot, wts = dta["dt"], dta["lndt"], dta["a"], dta["cs"], dta["tot"], dta["wts"]
        tt(DVE, dt[:], dtraw[:], bcm(dtb_t[:], nt), ALU.add, ["dtraw", "dtb"], ["dt"])
        act(dt[:], dt[:], AF.Exp, ["dt"], ["dt"])
        act(dt[:], dt[:], AF.Ln, ["dt", "ones"], ["dt"], bias=ones[:, 0:1])
        act(lndt[:], dt[:], AF.Ln, ["dt"], ["lndt"])
        stt(DVE, aa[:], dt[:], -1.0, bcm(expA[:], nt), ALU.mult, ALU.mult, ["dt", "expA"], ["a"])
        for i in range(nt):
            mm(bank(6)[:, i * 32:i * 32 + 16], Umat[:], aa[:, i, 0:16], True, True, ["Umat", "a"], [pk(6)])
            mm(bank(6)[:, i * 32 + 16:i * 32 + 32], Lmat[:], aa[:, i, 16:32], True, True, ["Lmat", "a"], [pk(6)])
            mm(bank(7)[:, i * 32:i * 32 + 32], ones[:], aa[:, i, :], True, True, ["ones", "a"], [pk(7)])
        c3 = bank(6)[:, 0:nt * 32].rearrange("p (a b) -> p a b", b=32)
        t3 = bank(7)[:, 0:nt * 32].rearrange("p (a b) -> p a b", b=32)
        cp(DVE, csa[:], c3, [pk(6)], ["cs"])
        cp(DVE, tot[:], t3, [pk(7)], ["tot"])
        tt(DVE, wts[:], tot[:], csa[:], ALU.subtract, ["tot", "cs"], ["wts"])
        tt(DVE, wts[:], wts[:], lndt[:], ALU.add, ["wts", "lndt"], ["wts"])
        act(wts[:], wts[:], AF.Exp, ["wts"], ["wts"])
        act(csa[:], csa[:], AF.Exp, ["cs"], ["cs"])
        act(tot[:], tot[:], AF.Exp, ["tot"], ["tot"])

    def state_pass(nt, order, col0, xs_tok, B_tok, xbcT, dta, S, Sbf, xw, tmpf, ybuf=None, ykey=None):
        for c in order:
            tsl = slice(2 + c * 128, 2 + (c + 1) * 128)
            if ybuf is not None:
                for g in range(2):
                    mm(bank(g), xbcT[:, 10 + g, tsl], Sbf[:, g * 512:(g + 1) * 512], True, True,
                       [("xbc", 10 + g), "Sbf"], [pk(g)])
                    tt(DVE, ybuf[:, c, g * 512:(g + 1) * 512].rearrange("p (h e) -> p h e", e=64),
                       bank(g).rearrange("p (h e) -> p h e", e=64),
                       bcl(dta["cs"][:, c, col0 + g * 8:col0 + g * 8 + 8], 64), ALU.mult, [pk(g), "cs"], [(ykey, c)])
            x2 = xw[c % 2]
            tt(DVE, x2[:].rearrange("p (h e) -> p h e", e=64), xs_tok[:, c, :].rearrange("p (h e) -> p h e", e=64),
               bcl(dta["wts"][:, c, col0:col0 + 16], 64), ALU.mult, [("xs", c), "wts"], [("xw", c % 2)])
            for g in range(2):
                mm(bank(2 + g), B_tok[:, c, g * 128:(g + 1) * 128], x2[:, g * 512:(g + 1) * 512], True, True,
                   ["Btok", ("xw", c % 2)], [pk(2 + g)])
            tt(DVE, tmpf[:].rearrange("p (h e) -> p h e", e=64), S[:].rearrange("p (h e) -> p h e", e=64),
               bcl(dta["tot"][:, c, col0:col0 + 16], 64), ALU.mult, ["S", "tot"], ["tmpS"])
            for g in range(2):
                tt(DVE, S[:, g * 512:(g + 1) * 512], tmpf[:, g * 512:(g + 1) * 512], bank(2 + g), ALU.add,
                   ["tmpS", pk(2 + g)], ["S"])
            cp(ACT, Sbf[:], S[:], ["S"], ["Sbf"])

    xbcT_t, _ = sbt("xbcT", [128, 12, T + 4], BF16, RA)
    RBt, _ = sbt("RB", [128, 8 * T], BF16, RB)
    hlT = RBt[:].rearrange("p (k t) -> p k t", k=8)
    xs_tok = RBt[:].rearrange("p (i f) -> p i f", f=D)
    SA = Alloc(ST)
    Hs = SA("Hs", [128, D]); Gs = SA("Gs", [128, D])
    Hbf = SA("Hbf", [128, D], BF16); Gbf = SA("Gbf", [128, D], BF16)
    dtraw = SA("dtraw", [128, NT, 32])
    assert SA.off <= WB

    for b in range(NB):
        A1 = Alloc(WB + 8 * NA * 2)
        if b > 0:
            dma(POOL, wA[:], w_in.rearrange("(k p) n -> p k n", p=128)[:, :, X0:X0 + NA], [], ["wA"])
        markW = A1.off
        hlT_c = A1("hlT_c", [128, 8, TC], BF16)
        xbcT_c = A1("xbcT_c", [128, 12, TC + 4], BF16)
        xs_c = A1("xs_c", [128, NTC, D], BF16)
        B_c = A1("B_c", [128, NTC, 256], BF16)
        dtraw_c = A1("dtraw_c", [128, NTC, 32])
        dta_c = {n: A1("c_" + n, [128, NTC, 32]) for n in ["dt", "lndt", "a", "cs", "tot", "wts"]}
        xseg = A1("xseg", [128, 4, D]); xn = A1("xn", [128, 4, D], BF16)
        junk = A1("junk", [128, D], BF16); ss = A1("ss", [128, 16])
        dg = A1("dg", [128, 60 * 128], BF16)
        xsh_c = A1("xsh_c", [128, 12, TC + 4], BF16)
        xw = [A1("xw%d" % i, [128, D], BF16) for i in range(2)]
        tmpf = A1("tmpf", [128, D])
        ms(POOL, xbcT_c[:, :, 0:2], 0.0, [("xbc", j) for j in range(12)])
        ms(POOL, xbcT_c[:, :, TC + 2:TC + 4], 0.0, [("xbc", j) for j in range(12)])
        norm_to_T(ctx[b], TC, scale1[NB], modT[NB], [("scale1", NB), ("modT", NB)], hlT_c, "hlT", xseg, xn, junk, ss)
        sweepA(hlT_c, TC, wA, xbcT_c, dtraw_c)
        dt_stage(TC, dtraw_c, dta_c)
        conv_stage(TC, xbcT_c, dg, xsh_c, xs_c, B_c)
        ms(DVE, Hs[:], 0.0, ["S"]); ms(DVE, Gs[:], 0.0, ["S"])
        state_pass(NTC, range(NTC), 0, xs_c, B_c, xbcT_c, dta_c, Hs, Hbf, xw, tmpf)
        P.barrier()
        state_pass(NTC, range(NTC - 1, -1, -1), 16, xs_c, B_c, xbcT_c, dta_c, Gs, Gbf, xw, tmpf)
        P.barrier()

        A2 = Alloc(markW)
        wb1 = A2("wb1", [128, 8, 2048], BF16)
        mark = A2.off
        xseg = [A2("xseg%d" % i, [128, 4, D]) for i in range(2)]
        xn = A2("xn", [128, 4, D], BF16)
        junk = A2("junk", [128, D], BF16); ss = [A2("ss%d" % i, [128, 16]) for i in range(2)]
        dma(POOL, wb1[:], w_in.rearrange("(k p) n -> p k n", p=128)[:, :, 0:2048], [], ["wb1"])
        norm_to_T(x[b], T, scale1[b], modT[b], [("scale1", b), ("modT", b)], hlT, "hlT", xseg, xn, junk, ss)
        P.barrier()

        A3 = Alloc(mark)
        gnw_bc = A3("gnw_bc", [128, D]); bs_bc = A3("bs_bc", [128, D])
        dma(SP, gnw_bc[:], prow(gnw), [], ["gnw_bc"]); dma(SP, bs_bc[:], prow(bsv), [], ["bs_bc"])
        uT = A3("uT", [128, 8, 512], BF16); vv = A3("vv", [128, 4, D], BF16)
        sq = A3("sq", [128, D], BF16); vnf = A3("vnf", [128, D]); vn2 = A3("vn2", [128, D], BF16)
        mxt = A3("mxt", [128, D]); gmT = [A3("gmT%d" % i, [128, D], BF16) for i in range(2)]
        zt = [A3("zt%d" % i, [128, D], BF16) for i in range(2)]
        tz = A3("tz", [128, 512]); ssv = A3("ssv", [128, 96])
        rr = 0
        for s0 in range(0, T, 512):
            for h in range(8):
                bk = 4 + rr % 4; rr += 1
                for k in range(8):
                    mm(bank(bk), wb1[:, k, h * 128:(h + 1) * 128], hlT[:, k, s0:s0 + 512], k == 0, k == 7, ["wb1", "hlT"], [pk(bk)])
                act(uT[:, h, :], bank(bk), AF.Gelu, [pk(bk)], ["uT"])
            for i in range(4):
                t0 = s0 + i * 128
                for hf in range(2):
                    bk = 4 + rr % 4; rr += 1
                    for k in range(8):
                        mm(bank(bk), hlT[:, k, t0:t0 + 128], wb1[:, k, 1024 + hf * 512:1024 + (hf + 1) * 512], k == 0, k == 7,
                           ["wb1", "hlT"], [pk(bk)])
                    act(vv[:, i, hf * 512:(hf + 1) * 512], bank(bk), AF.Gelu, [pk(bk)], [("vv", i)])
                tt(DVE, sq[:], vv[:, i, :], vv[:, i, :], ALU.mult, [("vv", i)], ["sq"])
                P.add(DVE, (lambda o, a: (lambda h_: h_.tensor_reduce(out=o, in_=a, axis=AX.X, op=ALU.add)))(
                    ssv[:, i * 8:(i + 1) * 8], sq[:].rearrange("p (h e) -> p h e", e=128)), ["sq"], ["ssv"])
            act(ssv[:, 32:64], ssv[:, 0:32], AF.Ln, ["ssv", "eps1"], ["ssv"], bias=eps1[:], scale=1.0 / 128)
            act(ssv[:, 64:96], ssv[:, 32:64], AF.Exp, ["ssv"], ["rsv"], scale=-0.5)
            for i in range(4):
                t0 = s0 + i * 128
                tt(DVE, vnf[:].rearrange("p (h e) -> p h e", e=128), vv[:, i, :].rearrange("p (h e) -> p h e", e=128),
                   bcl(ssv[:, 64 + i * 8:64 + (i + 1) * 8], 128), ALU.mult, [("vv", i), "rsv"], ["vnf"])
                tt(DVE, vn2[:], vnf[:], gnw_bc[:], ALU.mult, ["vnf", "gnw_bc"], ["vn2"])
                for h in range(8):
                    mm(bank(h // 4)[:, (h % 4) * 128:(h % 4 + 1) * 128], vn2[:, h * 128:(h + 1) * 128], WsT[:, h, :], True, True,
                       ["vn2", "WsT"], [pk(h // 4)])
                g_ = gmT[i % 2]
                for hf in range(2):
                    tt(DVE, mxt[:, hf * 512:(hf + 1) * 512], bank(hf), bs_bc[:, hf * 512:(hf + 1) * 512], ALU.add,
                       [pk(hf), "bs_bc"], ["mxt"])
                tt(DVE, g_[:].rearrange("p (h e) -> p h e", e=128), mxt[:].rearrange("p (h e) -> p h e", e=128),
                   uT[:, :, i * 128:(i + 1) * 128], ALU.mult, ["mxt", "uT"], [("gmT", i % 2)])
                dma(POOL, gm_s[t0 // 128], g_[:], [("gmT", i % 2)], [("gm_s", t0 // 128)])
        ms(POOL, xbcT_t[:, :, 0:2], 0.0, [("xbc", j) for j in range(12)])
        ms(POOL, xbcT_t[:, :, T + 2:T + 4], 0.0, [("xbc", j) for j in range(12)])
        sweepA(hlT, T, wA, xbcT_t, dtraw)
        dma(POOL, wb1[:, :, 0:1024], w_in.rearrange("(k p) n -> p k n", p=128)[:, :, Z0:Z0 + 1024], [], ["wb1"])
        for i in range(NT):
            t0 = i * 128
            z_ = zt[i % 2]
            for hf in range(2):
                bk = 4 + rr % 4; rr += 1
                for k in range(8):
                    mm(bank(bk), hlT[:, k, t0:t0 + 128], wb1[:, k, hf * 512:(hf + 1) * 512], k == 0, k == 7, ["wb1", "hlT"], [pk(bk)])
                act(tz[:], bank(bk), AF.Tanh, [pk(bk)], ["tz"], scale=0.5)
                stt(DVE, z_[:, hf * 512:(hf + 1) * 512], tz[:], 1.0, bank(bk), ALU.add, ALU.mult, ["tz", pk(bk)], [("zt", i % 2)])
            dma(POOL, z_s[t0:t0 + 128, :], z_[:], [("zt", i % 2)], [("z_s", i)])
        P.barrier()

        A4 = Alloc(WB)
        B_tok = A4("B_tok", [128, NT, 256], BF16)
        dta = {n: A4("l_" + n, [128, NT, 32]) for n in ["dt", "lndt", "a", "cs", "tot", "wts"]}
        dl = A4("dl", [128, NT, 16]); coef = A4("coef", [128, NT, 16]); cbd = A4("cbd", [128, NT, 2])
        mark5 = A4.off
        dg = A4("dg", [128, 60 * 128], BF16)
        xsh = A4("xsh", [128, 12, T + 4], BF16)
        dt_stage(T, dtraw, dta)
        conv_stage(T, xbcT_t, dg, xsh, xs_tok, B_tok)
        tt(DVE, dl[:], dta["lndt"][:, :, 16:32], dta["lndt"][:, :, 0:16], ALU.subtract, ["lndt"], ["dl"])
        P.barrier()

        A5 = Alloc(mark5)
        xw = [A5("xw%d" % i, [128, D], BF16) for i in range(2)]
        tmpf = A5("tmpf", [128, D])
        ybt, _ = sbt("ybuf", [128, NT, D], BF16, RA)
        state_pass(NT, range(NT - 1, -1, -1), 16, xs_tok, B_tok, xbcT_t, dta, Gs, Gbf, xw, tmpf, ybuf=ybt, ykey="yb")
        P.barrier()

        rf_ = A5("rf", [128, 16 * 128], BF16); rb_ = A5("rb", [128, 16 * 128], BF16)
        rf = [rf_, rf_]; rb = [rb_, rb_]
        wo, _ = sbt("wo", [128, 16, D], BF16, 180032)
        dma(POOL, wo[:], w_out.rearrange("(k p) n -> p k n", p=128), [], ["wo"])
        Mf = [A5("Mf%d" % i, [128, 16 * 128], BF16) for i in range(2)]
        Mb = [A5("Mb%d" % i, [128, 16 * 128], BF16) for i in range(2)]
        E4_ = A5("E4", [128, 512], BF16); E4 = [E4_, E4_]
        t1b = [A5("t1b%d" % i, [128, D], BF16) for i in range(2)]
        xdf = [A5("xdf%d" % i, [128, D], BF16) for i in range(2)]
        xdb = [A5("xdb%d" % i, [128, D], BF16) for i in range(2)]
        xc = xw[1]
        cbf = A5("cbf", [128, 256], BF16); cbb = A5("cbb", [128, 256], BF16); junkf = A5("junkf", [128, 128])
        t1 = A5("t1", [128, D], BF16); sz_ = A5("sz", [128, D], BF16); szt = [sz_, sz_]
        gn = xw[1]; ssg = A5("ssg", [128, 8]); junk = xc
        assert A5.off <= 180032, A5.off

        def v64(a):
            return a.rearrange("p (h e) -> p h e", e=64)

        def A_yoff(c):
            tsl = slice(2 + c * 128, 2 + (c + 1) * 128)
            for g in range(2):
                mm(bank(g), xbcT_t[:, 10 + g, tsl], Hbf[:, g * 512:(g + 1) * 512], True, True, [("xbc", 10 + g), "Sbf"], [pk(g)])

        def A_dve_nodep(c):
            p = c % 2
            tt(DVE, v64(xw[0][:]), v64(xs_tok[:, c, :]), bcl(dta["wts"][:, c, 0:16], 64), ALU.mult, [("xs", c), "wts"], [("xw", 0)])
            tt(DVE, v64(tmpf[:]), v64(Hs[:]), bcl(dta["tot"][:, c, 0:16], 64), ALU.mult, ["S", "tot"], ["tmpS"])
            tt(DVE, v64(xdf[p][:]), v64(xs_tok[:, c, :]), bcl(dta["dt"][:, c, 0:16], 64), ALU.mult, [("xs", c), "dt"], [("xdf", p)])
            tt(DVE, v64(xdb[p][:]), v64(xs_tok[:, c, :]), bcl(dta["dt"][:, c, 16:32], 64), ALU.mult, [("xs", c), "dt"], [("xdb", p)])

        def A_pe2(c):
            tsl = slice(2 + c * 128, 2 + (c + 1) * 128)
            for g in range(2):
                mm(bank(3 + g), B_tok[:, c, g * 128:(g + 1) * 128], xw[0][:, g * 512:(g + 1) * 512], True, True,
                   ["Btok", ("xw", 0)], [pk(3 + g)])
            for g in range(2):
                mm(bank(2)[:, g * 128:(g + 1) * 128], xbcT_t[:, 8 + g, tsl], xbcT_t[:, 10 + g, tsl], True, True,
                   [("xbc", 8 + g), ("xbc", 10 + g)], [pk(2)])

        def A_dve_dep(c):
            p = c % 2
            for g in range(2):
                tt(DVE, v64(t1b[p][:, g * 512:(g + 1) * 512]), v64(bank(g)),
                   bcl(dta["cs"][:, c, g * 8:g * 8 + 8], 64), ALU.mult, [pk(g), "cs"], [("t1b", p)])
            for g in range(2):
                tt(DVE, Hs[:, g * 512:(g + 1) * 512], tmpf[:, g * 512:(g + 1) * 512], bank(3 + g), ALU.add, ["tmpS", pk(3 + g)], ["S"])
            cp(ACT, Hbf[:], Hs[:], ["S"], ["Sbf"])
            c3 = bank(2)[:, 0:256].rearrange("p (g t) -> p g t", t=128)
            tt(DVE, cbf[:].rearrange("p (g t) -> p g t", t=128), c3, bcm(Umat[:], 2), ALU.mult, [pk(2), "Umat"], ["cbf"])
            tt(DVE, cbb[:].rearrange("p (g t) -> p g t", t=128), c3, bcm(mLs[:], 2), ALU.mult, [pk(2), "mLs"], ["cbb"])
            for g in range(2):
                tt(DVE, junkf[:], bank(2)[:, g * 128:(g + 1) * 128], identf[:], ALU.mult, [pk(2), "identf"], ["junkf"])
                P.add(DVE, (lambda o, a_: (lambda h_: h_.tensor_reduce(out=o, in_=a_, axis=AX.X, op=ALU.add)))(
                    cbd[:, c, g:g + 1], junkf[:]), ["junkf"], [("cbd", c)])

        def A_decay(c):
            p = c % 2
            for h in range(16):
                ts(POOL, rf[p][:, h * 128:(h + 1) * 128], Ubf[:], dta["a"][:, c, h:h + 1], None, ALU.mult, None,
                   ["Ubf", "a"], [("rf", 0, h // 4)])
                ts(POOL, rb[p][:, h * 128:(h + 1) * 128], Lbf[:], dta["a"][:, c, 16 + h:17 + h], None, ALU.mult, None,
                   ["Lbf", "a"], [("rb", 0, h // 4)])
            for q in range(4):
                bk = 3 + q % 2
                mm(bank(bk), SLf[:], rf[p][:, q * 512:(q + 1) * 512], True, False, ["SLf", ("rf", 0, q)], [pk(bk)])
                mm(bank(bk), SLb[:], rb[p][:, q * 512:(q + 1) * 512], False, True, ["SLb", ("rb", 0, q)], [pk(bk)])
                act(E4[q % 2][:], bank(bk), AF.Exp, [pk(bk)], [("E4", 0)])
                g = q // 2
                tt(DVE, Mf[p][:, q * 512:(q + 1) * 512].rearrange("p (h t) -> p h t", t=128),
                   E4[q % 2][:].rearrange("p (h t) -> p h t", t=128), bcm(cbf[:, g * 128:(g + 1) * 128], 4), ALU.mult,
                   [("E4", 0), "cbf"], [("Mf", p, q)])
                tt(DVE, Mb[p][:, q * 512:(q + 1) * 512].rearrange("p (h t) -> p h t", t=128),
                   E4[q % 2][:].rearrange("p (h t) -> p h t", t=128), bcm(cbb[:, g * 128:(g + 1) * 128], 4), ALU.mult,
                   [("E4", 0), "cbb"], [("Mb", p, q)])

        def B_front(c):
            p = c % 2
            for g in range(2):
                stt(DVE, coef[:, c, g * 8:(g + 1) * 8], dta["dt"][:, c, 16 + g * 8:16 + (g + 1) * 8], cbd[:, c, g:g + 1],
                    dD_t[:, g * 8:(g + 1) * 8], ALU.mult, ALU.add, ["dt", ("cbd", c), "dD"], [("coef", c)])
            tt(DVE, v64(xc[:]), v64(xs_tok[:, c, :]), bcl(coef[:, c, :], 64), ALU.mult, [("xs", c), ("coef", c)], [("xw", 1)])
            for hb in range(2):
                bk = 5 + hb
                cs_ = slice(hb * 512, (hb + 1) * 512)
                mm(bank(bk), ident[:], t1b[p][:, cs_], True, False, ["ident", ("t1b", p)], [pk(bk)])
                mm(bank(bk), ident[:], ybt[:, c, cs_], False, False, ["ident", ("yb", c)], [pk(bk)])
                mm(bank(bk), ident[:], xc[:, cs_], False, False, ["ident", ("xw", 1)], [pk(bk)])
                for hh in range(8):
                    H = hb * 8 + hh
                    o = bank(bk)[:, hh * 64:(hh + 1) * 64]
                    mm(o, Mf[p][:, H * 128:(H + 1) * 128], xdf[p][:, H * 64:(H + 1) * 64], False, False,
                       [("Mf", p, H // 4), ("xdf", p)], [pk(bk)])
                    mm(o, Mb[p][:, H * 128:(H + 1) * 128], xdb[p][:, H * 64:(H + 1) * 64], False, hh == 7,
                       [("Mb", p, H // 4), ("xdb", p)], [pk(bk)])

        def B_gate(c):
            for hb in range(2):
                cs_ = slice(hb * 512, (hb + 1) * 512)
                tt(DVE, t1[:, cs_], bank(5 + hb), szt[0][:, cs_], ALU.mult, [pk(5 + hb), "sz"], ["t1g"])

        def B_norm(c):
            ms(DVE, ssg[:, 0:2], 0.0, ["ssg"])
            for g in range(2):
                act(junk[:, 0:512], t1[:, g * 512:(g + 1) * 512], AF.Square, ["t1g"], [("xw", 1), "ssg"], accum=ssg[:, g:g + 1])
            act(ssg[:, 2:4], ssg[:, 0:2], AF.Ln, ["ssg", "eps4"], ["ssg"], bias=eps4[:], scale=1.0 / 512)
            act(ssg[:, 4:6], ssg[:, 2:4], AF.Exp, ["ssg"], ["rsg"], scale=-0.5)
            for g in range(2):
                act(gn[:, g * 512:(g + 1) * 512], t1[:, g * 512:(g + 1) * 512], AF.Identity, ["t1g", "rsg"], [("xw", 1)],
                    scale=ssg[:, 4 + g:5 + g])

        def B_out(c):
            pv = bankb(7)
            for k in range(8):
                tr(pv[:, k * 128:(k + 1) * 128], gn[:, k * 128:(k + 1) * 128], [("xw", 1)], [pk(7)])
            cp(ACT, ybt[:, c, :], pv, [pk(7)], [("yb", c)])
            if c + 1 < NT:
                dma(SP, sz_[:], z_s[(c + 1) * 128:(c + 2) * 128, :], [("z_s", c + 1)], ["sz"])

        dma(SP, sz_[:], z_s[0:128, :], [("z_s", 0)], ["sz"])
        A_yoff(0); A_dve_nodep(0); A_pe2(0); A_dve_dep(0); A_decay(0)
        for c in range(NT):
            nx = c + 1 < NT
            B_front(c)
            if nx:
                A_yoff(c + 1)
                A_dve_nodep(c + 1)
                A_pe2(c + 1)
            B_gate(c)
            B_norm(c)
            if nx:
                A_dve_dep(c + 1)
            B_out(c)
            if nx:
                A_decay(c + 1)
        P.barrier()

        A7 = Alloc(155456)
        w13t, _ = sbt("w13t", [128, 8, 5632], BF16, 41984)
        g1_bc = A7("g1_bc", [128, D])
        xt7 = [A7("xt7_%d" % i, [128, D]) for i in range(2)]
        gmt7 = [A7("gmt7_%d" % i, [128, D], BF16) for i in range(2)]
        x1t = [A7("x1t_%d" % i, [128, D]) for i in range(2)]
        assert A7.off <= 180032, A7.off
        dma(POOL, w13t[:], w13.rearrange("(k p) n -> p k n", p=128), [], ["w13"])
        dma(SP, g1_bc[:], prow(m_s[b:b + 1, 2 * D:3 * D]), ["m_s"], ["g1_bc"])
        for k in range(16):
            if k >= 8:
                ts(DVE, wo[:, k, :], wo[:, k, :], snw_t[:, k - 8:k - 7], None, ALU.mult, None, ["wo", "snw"], [("wok", k)])
            if k < 8:
                pass
        for c in range(NT):
            t0 = c * 128
            dma(SP, xt7[c % 2][:], x[b, t0:t0 + 128, :], [], [("xt7", c % 2)])
            dma(SP, gmt7[c % 2][:], gm_s[c], [("gm_s", c)], [("gmt7", c % 2)])
            for hf in range(2):
                bk = (c % 2) * 2 + hf
                for k in range(16):
                    lhs = gmt7[c % 2][:, k * 128:(k + 1) * 128] if k < 8 else ybt[:, c, (k - 8) * 128:(k - 7) * 128]
                    mm(bank(bk), lhs, wo[:, k, hf * 512:(hf + 1) * 512], k == 0, k == 15,
                       [("gmt7", c % 2), ("yb", c), "wo" if k < 8 else ("wok", k)], [pk(bk)])
                tt(DVE, x1t[c % 2][:, hf * 512:(hf + 1) * 512], bank(bk), g1_bc[:, hf * 512:(hf + 1) * 512], ALU.mult,
                   [pk(bk), "g1_bc"], [("x1t", c % 2)])
                tt(DVE, x1t[c % 2][:, hf * 512:(hf + 1) * 512], x1t[c % 2][:, hf * 512:(hf + 1) * 512],
                   xt7[c % 2][:, hf * 512:(hf + 1) * 512], ALU.add, [("xt7", c % 2), ("x1t", c % 2)], [("x1t", c % 2)])
            dma(POOL, x1_s[t0:t0 + 128, :], x1t[c % 2][:], [("x1t", c % 2)], [("x1_s", c)])
        P.barrier()

        import os as _os
        if "8" in _os.environ.get("KSKIP", ""):
            continue
        A8 = Alloc(RA)
        xseg = A8("xseg", [128, 4, D]); xn = A8("xn", [128, 4, D], BF16)
        hl2T = A8("hl2T", [128, 8, 512], BF16)
        assert A8.off <= 41984, A8.off
        w2t, _ = sbt("w2t", [128, 22, D], BF16, 132096)
        A8 = Alloc(177152)
        hT = A8("hT", [128, 22, 512], BF16)
        sg = A8("sg", [128, 512]); fnw_bc = A8("fnw_bc", [128, D]); yo = A8("yo", [128, D])
        junk = A8("junk", [128, D], BF16); ss = A8("ss", [128, 16]); ss3 = A8("ss3", [128, 16])
        g2t = yo
        dma(POOL, w2t[:], w2.rearrange("(k p) n -> p k n", p=128), [], ["w2"])
        dma(SP, g2t[:], prow(m_s[b:b + 1, 5 * D:6 * D]), ["m_s"], ["g2_bc"])
        dma(SP, fnw_bc[:], prow(fnw), [], ["fnw_bc"])
        for s0 in range(0, T, 512):
            norm_to_T(x1_s[s0:s0 + 512, :], 512, scale2[b], modT[b][:, 24:32], [("scale2", b), ("modT", b)], hl2T, "hl2T",
                      xseg, xn, junk, ss)
            for j in range(22):
                bg, bu = 4 + (j % 2) * 2, 5 + (j % 2) * 2
                for k in range(8):
                    mm(bank(bg), w13t[:, k, j * 128:(j + 1) * 128], hl2T[:, k, :], k == 0, k == 7, ["w13", "hl2T"], [pk(bg)])
                for k in range(8):
                    mm(bank(bu), w13t[:, k, 2816 + j * 128:2816 + (j + 1) * 128], hl2T[:, k, :], k == 0, k == 7, ["w13", "hl2T"], [pk(bu)])
                act(sg[:], bank(bg), AF.Silu, [pk(bg)], ["sg"])
                tt(DVE, hT[:, j, :], sg[:], bank(bu), ALU.mult, ["sg", pk(bu)], [("hT", j)])
            if s0 == 0:
                for k in range(22):
                    tt(DVE if k % 2 == 0 else POOL, w2t[:, k, :], w2t[:, k, :], g2t[:], ALU.mult,
                       ["w2", ("w2k", k), "g2_bc"], [("w2k", k), "yo"])
            ms(DVE, ss3[:, 0:8], 0.0, ["ss3"])
            for i in range(4):
                for hf in range(2):
                    bk = hf
                    for j in range(22):
                        mm(bank(bk), hT[:, j, i * 128:(i + 1) * 128], w2t[:, j, hf * 512:(hf + 1) * 512], j == 0, j == 21,
                           [("hT", j), ("w2k", j)], [pk(bk)])
                    tt(DVE, xseg[:, i, hf * 512:(hf + 1) * 512], xseg[:, i, hf * 512:(hf + 1) * 512], bank(bk), ALU.add,
                       [("xseg", 0, i), pk(bk)], [("xseg", 0, i)])
                act(junk[:], xseg[:, i, :], AF.Square, [("xseg", 0, i)], ["junk", "ss3"], accum=ss3[:, i:i + 1])
            act(ss3[:, 4:8], ss3[:, 0:4], AF.Ln, ["ss3", "eps1"], ["ss3"], bias=eps1[:], scale=1.0 / D)
            act(ss3[:, 8:12], ss3[:, 4:8], AF.Exp, ["ss3"], ["rs3"], scale=-0.5)
            for i in range(4):
                stt(DVE, yo[:], xseg[:, i, :], ss3[:, 8 + i:9 + i], fnw_bc[:], ALU.mult, ALU.mult, [("xseg", 0, i), "rs3", "fnw_bc"], ["yo"])
                dma(POOL, out[b, s0 + i * 128:s0 + (i + 1) * 128, :], yo[:], ["yo"], [("out", s0 // 128 + i)])
        P.barrier()

    run_prog(nc, P)
    return nc


def _prep_common(inp):
    f = np.float32
    g = lambda k: np.asarray(inp[k], dtype=f)
    fm = lambda v: np.ascontiguousarray(v.reshape(-1, 128).T)
    cm = {
        "w_ada": np.ascontiguousarray(g("w_ada")[0]), "b_ada": np.ascontiguousarray(g("b_ada")[0].reshape(1, -1)),
        "w_in": np.ascontiguousarray(g("w_in")[0]), "w_out": np.ascontiguousarray(g("w_out")[0]),
        "w13": np.ascontiguousarray(g("ffn_w13")[0]), "w2": np.ascontiguousarray(g("ffn_w2")[0]),
        "n1w": fm(g("norm1_w")[0]), "n2w": fm(g("norm2_w")[0]), "snw": fm(g("ssd_norm_w")[0]),
        "gnw": np.ascontiguousarray(g("gm_norm_w")[0].reshape(1, -1)),
        "bsv": np.ascontiguousarray(g("gm_bs")[0].reshape(1, -1)),
        "fnw": np.ascontiguousarray(g("final_norm_w").reshape(1, -1)),
        "wsT": np.ascontiguousarray(g("gm_ws")[0].transpose(2, 0, 1)),
        "cw": np.ascontiguousarray(g("conv_w")[0].reshape(5, 12, 128).transpose(2, 1, 0)),
        "cbv": fm(g("conv_b")[0]),
        "alog": np.ascontiguousarray(g("ssd_A_log")[0].reshape(1, 32)),
        "dtbias": np.ascontiguousarray(g("ssd_dt_bias")[0].reshape(1, 32)),
        "dD": np.ascontiguousarray(g("ssd_D")[0].reshape(1, 16)),
    }
    return cm


_NC_CACHE = {}


def run_cores(inp, ncores, NB):
    x = np.asarray(inp["x"], dtype=np.float32)
    ctx = np.asarray(inp["ctx"], dtype=np.float32)
    c = np.asarray(inp["c"], dtype=np.float32)
    c_ctx = np.asarray(inp["c_ctx"], dtype=np.float32)
    T, TC = x.shape[1], ctx.shape[1]
    key = (T, TC, NB)
    if key not in _NC_CACHE:
        _NC_CACHE[key] = build(T, TC, NB)
    nc = _NC_CACHE[key]
    cm = _prep_common(inp)
    maps = []
    for i in range(ncores):
        sl = slice(i * NB, (i + 1) * NB)
        vecs = np.concatenate([c[sl], c_ctx[None, :]], axis=0)
        cT = np.ascontiguousarray(vecs.reshape(NB + 1, 8, 128).transpose(2, 1, 0))
        m = dict(cm)
        m["x"] = np.ascontiguousarray(x[sl]); m["ctx"] = np.ascontiguousarray(ctx[sl]); m["cT"] = cT
        maps.append(m)
    res = run_bass_kernel_spmd(nc, maps, core_ids=list(range(ncores)))
    return np.concatenate([np.asarray(r["out"]) for r in res.results], axis=0).astype(np.float32)


def kernel(**inputs):
    return run_cores(inputs, 8, 2)
```

```python
from concourse.bass_utils import run_bass_kernel_spmd
import numpy as np
import concourse.bass as bass
import concourse.mybir as mybir

F32 = mybir.dt.float32
BF16 = mybir.dt.bfloat16
AF = mybir.ActivationFunctionType
ALU = mybir.AluOpType
AX = mybir.AxisListType

PE, ACT, DVE, POOL, SP = "pe", "act", "dve", "pool", "sp"
ENGS = [PE, ACT, DVE, POOL, SP]
N_DMA_SEMS = 8


class Op:
    __slots__ = ("eng", "fn", "reads", "writes", "dma", "idx", "deps", "signal",
                 "sigval", "dsem", "dval", "guard")

    def __init__(self, eng, fn, reads, writes, dma):
        self.eng, self.fn, self.reads, self.writes, self.dma = eng, fn, reads, writes, dma
        self.deps = []
        self.signal = False
        self.sigval = 0
        self.dsem = None
        self.dval = 0
        self.guard = None


class Prog:
    def __init__(self, nc):
        self.nc = nc
        self.ops = []
        self.last_w = {}
        self.readers = {}
        self.dma_rr = {e: 0 for e in ENGS}
        self.dma_last = {}
        self.last_eng = {}
        self.bar = {}

    def add(self, eng, fn, reads=(), writes=(), dma=False):
        reads = list(reads)
        writes = list(writes)
        ps_r = [k for k in reads if isinstance(k, str) and k.startswith("ps")]
        if ps_r:
            reads = [k for k in reads if k not in ps_r]
            writes = writes + [k for k in ps_r if k not in writes]
        op = Op(eng, fn, reads, writes, dma)
        op.idx = len(self.ops)
        deps = {}

        def dep(p, kind):
            if p is None:
                return
            if p.dma or dma:
                deps[p.idx] = p
                return
            if p.eng == eng:
                if eng == PE:
                    return
                deps[p.idx] = p
                return
            deps[p.idx] = p

        for k in reads:
            dep(self.last_w.get(k), "raw")
        for k in writes:
            dep(self.last_w.get(k), "waw")
            for r in self.readers.get(k, ()):
                dep(r, "war")
        for k in reads:
            lst = self.readers.setdefault(k, [])
            if not dma and eng != POOL:
                lst[:] = [r for r in lst if r.dma or r.eng != eng]
            lst.append(op)
        for k in writes:
            self.last_w[k] = op
            self.readers[k] = []
        for p in self.bar.pop(eng, ()):
            if p.dma or dma or p.eng != eng:
                deps[p.idx] = p
        op.deps = list(deps.values())
        if not dma:
            self.last_eng[eng] = op
        if dma:
            slot = self.dma_rr[eng] % N_DMA_SEMS
            self.dma_rr[eng] += 1
            op.dsem = (eng, slot)
            prev = self.dma_last.get(op.dsem)
            op.dval = (prev.dval if prev else 0) + 16
            op.guard = prev
            self.dma_last[op.dsem] = op
        self.ops.append(op)
        return op

    def barrier(self):
        pend = list(self.last_eng.values()) + list(self.dma_last.values())
        for e in ENGS:
            self.bar[e] = list(pend)

    def emit(self, sems, dsems):
        nc = self.nc
        for op in self.ops:
            for p in op.deps:
                if not p.dma:
                    p.signal = True
        cnt = {e: 0 for e in ENGS}
        for op in self.ops:
            if op.signal and not op.dma:
                cnt[op.eng] += 1
                op.sigval = cnt[op.eng]
        per = {e: [o for o in self.ops if o.eng == e] for e in ENGS}
        handles = {PE: nc.tensor, ACT: nc.scalar, DVE: nc.vector, POOL: nc.gpsimd, SP: nc.sync}
        self.n_waits = 0

        def run(eng, h=None):
            if h is None:
                h = handles[eng]
            waited = {}
            for op in per[eng]:
                need = {}
                for p in op.deps:
                    if p.dma:
                        key, val = ("d",) + p.dsem, p.dval
                    else:
                        key, val = ("e", p.eng), p.sigval
                    if need.get(key, 0) < val:
                        need[key] = val
                if op.guard is not None:
                    key = ("d",) + op.dsem
                    if need.get(key, 0) < op.guard.dval:
                        need[key] = op.guard.dval
                todo = []
                for key, val in need.items():
                    if waited.get(key, 0) >= val:
                        continue
                    waited[key] = val
                    todo.append((dsems[key[1:]] if key[0] == "d" else sems[key[1]], val))
                for s, val in todo[:-1]:
                    h.wait_ge(s, val)
                    self.n_waits += 1
                ins = op.fn(h)
                if todo:
                    ins._wait_ge(todo[-1][0], todo[-1][1])
                if op.dma:
                    ins.then_inc(dsems[op.dsem], 16)
                elif op.signal:
                    ins.then_inc(sems[eng], 1)
            for (e, slot), last in self.dma_last.items():
                if e == eng:
                    h.wait_ge(dsems[(e, slot)], last.dval)

        return run


def run_prog(nc, prog):
    from contextlib import ExitStack
    with ExitStack() as st:
        sems = {e: st.enter_context(nc.semaphore("s_" + e)) for e in ENGS}
        dsems = {}
        for e in ENGS:
            if prog.dma_rr[e] > 0:
                for i in range(N_DMA_SEMS):
                    dsems[(e, i)] = st.enter_context(nc.semaphore("d_%s%d" % (e, i)))
        run = prog.emit(sems, dsems)
        block = st.enter_context(nc.Block())

        @block.tensor
        def _(e):
            run(PE, e)

        @block.scalar
        def _(e):
            run(ACT, e)

        @block.vector
        def _(e):
            run(DVE, e)

        @block.gpsimd
        def _(e):
            run(POOL, e)

        @block.sync
        def _(e):
            run(SP, e)
D = 1024
EPS = 1e-6
U0, V0, Z0, X0, DT0 = 0, 1024, 2048, 3072, 4608
NA = 1568


def bcl(a, n):
    return bass.AP(a.tensor, a.offset, list(a.ap) + [[0, n]])


def bcm(a, n):
    return bass.AP(a.tensor, a.offset, [a.ap[0], [0, n]] + list(a.ap[1:]))


def prow(a, n=128):
    return bass.AP(a.tensor, a.offset, [[0, n]] + list(a.ap[1:]))


def build(T, TC, NB):
    nc = bass.Bass("TRN2", target_bir_lowering=False)
    NT, NTC = T // 128, TC // 128
    NJ = NB + 1

    def din(name, shape, dt=F32):
        return nc.dram_tensor(name, shape, dt, kind="ExternalInput").ap()

    x = din("x", [NB, T, D]); ctx = din("ctx", [NB, TC, D]); cT = din("cT", [128, 8, NJ])
    w_ada = din("w_ada", [D, 6 * D]); b_ada = din("b_ada", [1, 6 * D])
    w_in = din("w_in", [D, 4640]); w_out = din("w_out", [2 * D, D])
    w13 = din("w13", [D, 5632]); w2 = din("w2", [2816, D])
    n1w = din("n1w", [128, 8]); n2w = din("n2w", [128, 8]); snw = din("snw", [128, 8])
    gnw = din("gnw", [1, D]); bsv = din("bsv", [1, D]); fnw = din("fnw", [1, D])
    wsT = din("wsT", [128, 8, 128]); cw = din("cw", [128, 12, 5]); cbv = din("cbv", [128, 12])
    alog = din("alog", [1, 32]); dtbias = din("dtbias", [1, 32]); dD = din("dD", [1, 16])
    out = nc.dram_tensor("out", [NB, T, D], F32, kind="ExternalOutput").ap()
    m_s = nc.dram_tensor("m_s", [NJ, 6 * D], F32).ap()
    z_s = nc.dram_tensor("z_s", [T, D], BF16).ap()
    gm_s = nc.dram_tensor("gm_s", [T // 128, 128, D], BF16).ap()
    x1_s = nc.dram_tensor("x1_s", [T, D], F32).ap()

    P = Prog(nc)
    cnt = [0]

    def sbt(name, shape, dt, off):
        n = 1
        for s in shape[1:]:
            n *= s
        nb = n * (4 if dt == F32 else 2)
        cnt[0] += 1
        t = nc.alloc_sbuf_tensor_at("%s_%d" % (name, cnt[0]), shape, dt, offset=off + 16512)
        return t, off + ((nb + 31) // 32) * 32

    class Alloc:
        def __init__(self, off):
            self.off = off

        def __call__(self, name, shape, dt=F32):
            t, self.off = sbt(name, shape, dt, self.off)
            assert self.off <= 212800, (name, self.off)
            return t

    def mm(o, l, r, st, sp, R, W):
        P.add(PE, lambda h: h.matmul(o, l, r, start=st, stop=sp), R, W)

    def tr(o, i, R, W):
        P.add(PE, lambda h: h.transpose(o, i, ident[:]), list(R) + ["ident"], W)

    def act(o, i, f, R, W, bias=None, scale=None, accum=None):
        kw = {}
        if bias is not None:
            kw["bias"] = bias
        if scale is not None:
            kw["scale"] = scale
        if accum is not None:
            kw["accum_out"] = accum
        P.add(ACT, lambda h: h.activation(out=o, in_=i, func=f, **kw), R, W)

    def tt(e, o, a, b, op, R, W):
        P.add(e, lambda h: h.tensor_tensor(out=o, in0=a, in1=b, op=op), R, W)

    def ts(e, o, a, s1, s2, op0, op1, R, W):
        if s2 is None:
            P.add(e, lambda h: h.tensor_scalar(out=o, in0=a, scalar1=s1, scalar2=None, op0=op0), R, W)
        else:
            P.add(e, lambda h: h.tensor_scalar(out=o, in0=a, scalar1=s1, scalar2=s2, op0=op0, op1=op1), R, W)

    def stt(e, o, a, s, b, op0, op1, R, W, accum=None):
        if accum is None:
            P.add(e, lambda h: h.scalar_tensor_tensor(out=o, in0=a, scalar=s, in1=b, op0=op0, op1=op1), R, W)
        else:
            P.add(e, lambda h: h.scalar_tensor_tensor(out=o, in0=a, scalar=s, in1=b, op0=op0, op1=op1, accum_out=accum), R, W)

    def cp(e, o, i, R, W):
        if e == ACT:
            P.add(ACT, lambda h: h.copy(out=o, in_=i), R, W)
        else:
            P.add(e, lambda h: h.tensor_copy(out=o, in_=i), R, W)

    def ms(e, o, v, W):
        P.add(e, lambda h: h.memset(o, v), [], W)

    def dma(q, o, i, R, W, slow=False):
        if slow:
            P.add(q, lambda h: h.dma_start(out=o, in_=i, allow_slow_non_contiguous=True), R, W, dma=True)
        else:
            P.add(q, lambda h: h.dma_start(out=o, in_=i), R, W, dma=True)

    def asel(o, pat, cm, op, W):
        P.add(POOL, lambda h: h.affine_select(out=o, in_=o, pattern=pat, compare_op=op, fill=0.0,
                                              base=0, channel_multiplier=cm), W, W)

    pd = [nc.alloc_psum_tensor("pd%d" % i, [128, 1024], F32) for i in range(4)]

    def bank(i):
        return pd[i // 2][:, (i % 2) * 512:(i % 2 + 1) * 512]

    def bankb(i):
        return bank(i).bitcast(BF16)

    def pk(i):
        return "ps%d" % i

    CA = Alloc(0)
    ident = CA("ident", [128, 128], BF16); identf = CA("identf", [128, 128])
    Umat = CA("Umat", [128, 128]); Lmat = CA("Lmat", [128, 128]); ones = CA("ones", [128, 128])
    SLf = CA("SLf", [128, 128], BF16); SLb = CA("SLb", [128, 128], BF16)
    Ubf = CA("Ubf", [128, 128], BF16); Lbf = CA("Lbf", [128, 128], BF16)
    mLs = CA("mLs", [128, 128]); mtmp = CA("mtmp", [128, 128])
    WsT = CA("WsT", [128, 8, 128], BF16)
    cw_t = CA("cw", [128, 12, 5]); cb_t = CA("cb", [128, 12])
    n1w_t = CA("n1w", [128, 8]); n2w_t = CA("n2w", [128, 8]); snw_t = CA("snw", [128, 8])
    expA = CA("expA", [128, 32]); dtb_t = CA("dtb", [128, 32]); dD_t = CA("dD", [128, 16])
    eps1 = CA("eps1", [128, 1]); eps4 = CA("eps4", [128, 1])
    sc_t = CA("sc", [128, 8, NJ])
    modT = [CA("modT%d" % j, [128, 48]) for j in range(NJ)]
    scale1 = [CA("scale1_%d" % j, [128, 8]) for j in range(NJ)]
    scale2 = [CA("scale2_%d" % j, [128, 8]) for j in range(NJ)]
    CEND = CA.off
    assert CEND <= 9216, CEND
    RA, RB, ST, WB = 9216, 58496, 91264, 105600

    def mask(dst_f32, pat, cm, op, key):
        ms(POOL, dst_f32[:], 1.0, [key])
        asel(dst_f32[:], pat, cm, op, [key])
    mask(identf, [[1, 128]], -1, ALU.is_equal, "identf")
    mask(Umat, [[1, 128]], -1, ALU.is_ge, "Umat")
    mask(Lmat, [[-1, 128]], 1, ALU.is_ge, "Lmat")
    mask(mLs, [[-1, 128]], 1, ALU.is_gt, "mLs")
    mask(mtmp, [[1, 128]], -1, ALU.is_gt, "mtmp")
    ms(POOL, ones[:], 1.0, ["ones"])
    cp(DVE, ident[:], identf[:], ["identf"], ["ident"])
    cp(DVE, Ubf[:], Umat[:], ["Umat"], ["Ubf"])
    cp(DVE, Lbf[:], Lmat[:], ["Lmat"], ["Lbf"])
    cp(DVE, SLf[:], mLs[:], ["mLs"], ["SLf"])
    cp(DVE, SLb[:], mtmp[:], ["mtmp"], ["SLb"])
    ms(POOL, eps1[:], EPS, ["eps1"]); ms(POOL, eps4[:], 4 * EPS, ["eps4"])
    dma(POOL, WsT[:], wsT, [], ["WsT"])
    dma(SP, cw_t[:], cw, [], ["cw"]); dma(SP, cb_t[:], cbv, [], ["cb"])
    dma(SP, n1w_t[:], n1w, [], ["n1w"]); dma(SP, n2w_t[:], n2w, [], ["n2w"]); dma(SP, snw_t[:], snw, [], ["snw"])
    dma(SP, expA[:], prow(alog), [], ["expA"]); dma(SP, dtb_t[:], prow(dtbias), [], ["dtb"])
    dma(SP, dD_t[:], prow(dD), [], ["dD"])
    act(expA[:], expA[:], AF.Exp, ["expA"], ["expA"])
    dma(SP, sc_t[:], cT, [], ["sc"])
    act(sc_t[:], sc_t[:], AF.Silu, ["sc"], ["sc"])

    wA, _ = sbt("wA", [128, 8, NA], BF16, WB)
    dma(POOL, wA[:], w_in.rearrange("(k p) n -> p k n", p=128)[:, :, X0:X0 + NA], [], ["wA"])
    A0 = Alloc(RA)
    wblk = [A0("wblk%d" % i, [128, 3072]) for i in range(2)]
    bblk = A0("bblk", [NJ, 3072])
    mblk = A0("mblk", [NJ, 3072])
    it = 0
    for half in range(2):
        c0 = half * 3072
        dma(SP, bblk[:], prow(b_ada[:, c0:c0 + 3072], NJ), [], ["bblk"])
        for k in range(8):
            i = it % 2; it += 1
            dma(SP, wblk[i][:], w_ada[k * 128:(k + 1) * 128, c0:c0 + 3072], [], [("wblk", i)])
            for cb in range(6):
                mm(bank(cb)[0:NJ, :], sc_t[:, k, :], wblk[i][:, cb * 512:(cb + 1) * 512], k == 0, k == 7, ["sc", ("wblk", i)], [pk(cb)])
        for cb in range(6):
            tt(DVE, mblk[:, cb * 512:(cb + 1) * 512], bank(cb)[0:NJ, :], bblk[:, cb * 512:(cb + 1) * 512], ALU.add,
               [pk(cb), "bblk"], ["mblk"])
        dma(SP, m_s[:, c0:c0 + 3072], mblk[:], ["mblk"], ["m_s"])
    for j in range(NJ):
        dma(SP, modT[j][:], m_s[j:j + 1, :].rearrange("o (c p) -> p (o c)", p=128), ["m_s"], [("modT", j)], slow=True)
        stt(DVE, scale1[j][:], modT[j][:, 8:16], 1.0, n1w_t[:], ALU.add, ALU.mult, [("modT", j), "n1w"], [("scale1", j)])
        stt(DVE, scale2[j][:], modT[j][:, 32:40], 1.0, n2w_t[:], ALU.add, ALU.mult, [("modT", j), "n2w"], [("scale2", j)])
    P.barrier()

    def norm_to_T(src, L, scale_ap, shift_ap, mkeys, dst, dkey, xseg, xn, junk, sss, keep_key=None):
        xsegs = xseg if isinstance(xseg, list) else [xseg]
        xns = xn if isinstance(xn, list) else [xn]
        for s0 in range(0, L, 512):
            ln = min(512, L - s0)
            nt = ln // 128
            sp_ = (s0 // 512) % len(xsegs)
            xseg = xsegs[sp_]
            xn = xns[sp_ % len(xns)]
            xp_ = sp_ % len(xns)
            ss = sss[sp_] if isinstance(sss, list) else sss
            ms(DVE, ss[:, 0:8], 0.0, [("ss", sp_)])
            for i in range(nt):
                dma(SP, xseg[:, i, :], src[s0 + i * 128:s0 + (i + 1) * 128, :], [], [("xseg", sp_, i)])
                act(junk[:], xseg[:, i, :], AF.Square, [("xseg", sp_, i)], ["junk", ("ss", sp_)], accum=ss[:, i:i + 1])
            act(ss[:, 4:8], ss[:, 0:4], AF.Ln, [("ss", sp_), "eps1"], [("ss", sp_)], bias=eps1[:], scale=1.0 / D)
            act(ss[:, 8:12], ss[:, 4:8], AF.Exp, [("ss", sp_)], [("rstd", sp_)], scale=-0.5)
            for i in range(nt):
                if i % 2 == 0:
                    act(xn[:, i, :], xseg[:, i, :], AF.Identity, [("xseg", sp_, i), ("rstd", sp_)], [("xn", xp_, i)], scale=ss[:, 8 + i:9 + i])
                else:
                    ts(DVE, xn[:, i, :], xseg[:, i, :], ss[:, 8 + i:9 + i], None, ALU.mult, None,
                       [("xseg", sp_, i), ("rstd", sp_)], [("xn", xp_, i)])
            for k in range(8):
                pv = bankb(k // 2)[:, (k % 2) * 512:(k % 2) * 512 + ln]
                for i in range(nt):
                    tr(pv[:, i * 128:(i + 1) * 128], xn[:, i, k * 128:(k + 1) * 128], [("xn", xp_, i)], [pk(k // 2)])
                if k % 2 == 0:
                    act(dst[:, k, s0:s0 + ln], pv, AF.Identity, [pk(k // 2)] + mkeys, [dkey],
                        bias=shift_ap[:, k:k + 1], scale=scale_ap[:, k:k + 1])
                else:
                    ts(DVE, dst[:, k, s0:s0 + ln], pv, scale_ap[:, k:k + 1], shift_ap[:, k:k + 1], ALU.mult, ALU.add,
                       [pk(k // 2)] + mkeys, [dkey])

    def sweepA(hlT, L, wA, xbcT, dtraw):
        rr = 0
        for s0 in range(0, L, 512):
            ln = min(512, L - s0)
            for j in range(12):
                b = 4 + (rr % 4); rr += 1
                for k in range(8):
                    mm(bank(b)[:, 0:ln], wA[:, k, j * 128:(j + 1) * 128], hlT[:, k, s0:s0 + ln], k == 0, k == 7,
                       ["wA", "hlT"], [pk(b)])
                cp(ACT if j % 2 == 0 else DVE, xbcT[:, j, 2 + s0:2 + s0 + ln], bank(b)[:, 0:ln], [pk(b)], [("xbc", j)])
            for i in range(ln // 128):
                b = 4 + (rr % 4); rr += 1
                t0 = s0 + i * 128
                for k in range(8):
                    mm(bank(b)[:, 0:32], hlT[:, k, t0:t0 + 128], wA[:, k, 1536:1568], k == 0, k == 7, ["wA", "hlT"], [pk(b)])
                cp(DVE, dtraw[:, t0 // 128, :], bank(b)[:, 0:32], [pk(b)], ["dtraw"])

    def conv_stage(L, xbcT, dg, xsh, xs_tok, B_tok):
        nt = L // 128
        for j in range(12):
            for k in range(5):
                ts(DVE, dg[:, (j * 5 + k) * 128:(j * 5 + k + 1) * 128], ident[:], cw_t[:, j, k:k + 1], None, ALU.mult, None,
                   ["ident", "cw"], [("dg", j)])
            cp(POOL if j % 2 else DVE, xsh[:, j, 0:L + 2], xbcT[:, j, 1:L + 3], [("xbc", j)], [("xsh", j)])
        for j in range(12):
            segs = [(s0, min(512, L - s0)) for s0 in range(0, L, 512)]
            for si, (s0, ln) in enumerate(segs):
                bk = (j % 2) * 4 + si
                for k in range(5):
                    src = xbcT[:, j, s0 + k:s0 + k + ln] if k % 2 == 0 else xsh[:, j, s0 + k - 1:s0 + k - 1 + ln]
                    mm(bank(bk)[:, 0:ln], dg[:, (j * 5 + k) * 128:(j * 5 + k + 1) * 128], src,
                       k == 0, k == 4, [("dg", j), ("xbc", j), ("xsh", j)], [pk(bk)])
            for si, (s0, ln) in enumerate(segs):
                bk = (j % 2) * 4 + si
                act(xbcT[:, j, 2 + s0:2 + s0 + ln], bank(bk)[:, 0:ln], AF.Silu, [pk(bk), "cb"], [("xbc", j)], bias=cb_t[:, j:j + 1])
        for i in range(nt):
            pv = bankb(i % 2)
            for j in range(8):
                tr(pv[:, j * 128:(j + 1) * 128], xbcT[:, j, 2 + i * 128:2 + (i + 1) * 128], [("xbc", j)], [pk(i % 2)])
            cp(ACT if i % 2 == 0 else DVE, xs_tok[:, i, :], pv, [pk(i % 2)], [("xs", i)])
        for i0 in range(0, nt, 4):
            n = min(4, nt - i0)
            pv = bankb(2 + (i0 // 4) % 2)
            for ii in range(n):
                for g in range(2):
                    tr(pv[:, ii * 256 + g * 128:ii * 256 + (g + 1) * 128],
                       xbcT[:, 8 + g, 2 + (i0 + ii) * 128:2 + (i0 + ii + 1) * 128], [("xbc", 8 + g)], [pk(2 + (i0 // 4) % 2)])
            cp(DVE, B_tok[:, i0:i0 + n, :], pv[:, 0:n * 256].rearrange("p (a b) -> p a b", b=256),
               [pk(2 + (i0 // 4) % 2)], ["Btok"])

    def dt_stage(L, dtraw, dta):
        nt = L // 128
        dt, lndt, aa, csa, tot, wts = dta["dt"], dta["lndt"], dta["a"], dta["cs"], dta["tot"], dta["wts"]
        tt(DVE, dt[:], dtraw[:], bcm(dtb_t[:], nt), ALU.add, ["dtraw", "dtb"], ["dt"])
        act(dt[:], dt[:], AF.Exp, ["dt"], ["dt"])
        act(dt[:], dt[:], AF.Ln, ["dt", "ones"], ["dt"], bias=ones[:, 0:1])
        act(lndt[:], dt[:], AF.Ln, ["dt"], ["lndt"])
        stt(DVE, aa[:], dt[:], -1.0, bcm(expA[:], nt), ALU.mult, ALU.mult, ["dt", "expA"], ["a"])
        for i in range(nt):
            mm(bank(6)[:, i * 32:i * 32 + 16], Umat[:], aa[:, i, 0:16], True, True, ["Umat", "a"], [pk(6)])
            mm(bank(6)[:, i * 32 + 16:i * 32 + 32], Lmat[:], aa[:, i, 16:32], True, True, ["Lmat", "a"], [pk(6)])
            mm(bank(7)[:, i * 32:i * 32 + 32], ones[:], aa[:, i, :], True, True, ["ones", "a"], [pk(7)])
        c3 = bank(6)[:, 0:nt * 32].rearrange("p (a b) -> p a b", b=32)
        t3 = bank(7)[:, 0:nt * 32].rearrange("p (a b) -> p a b", b=32)
        cp(DVE, csa[:], c3, [pk(6)], ["cs"])
        cp(DVE, tot[:], t3, [pk(7)], ["tot"])
        tt(DVE, wts[:], tot[:], csa[:], ALU.subtract, ["tot", "cs"], ["wts"])
        tt(DVE, wts[:], wts[:], lndt[:], ALU.add, ["wts", "lndt"], ["wts"])
        act(wts[:], wts[:], AF.Exp, ["wts"], ["wts"])
        act(csa[:], csa[:], AF.Exp, ["cs"], ["cs"])
        act(tot[:], tot[:], AF.Exp, ["tot"], ["tot"])

    def state_pass(nt, order, col0, xs_tok, B_tok, xbcT, dta, S, Sbf, xw, tmpf, ybuf=None, ykey=None):
        for c in order:
            tsl = slice(2 + c * 128, 2 + (c + 1) * 128)
            if ybuf is not None:
                for g in range(2):
                    mm(bank(g), xbcT[:, 10 + g, tsl], Sbf[:, g * 512:(g + 1) * 512], True, True,
                       [("xbc", 10 + g), "Sbf"], [pk(g)])
                    tt(DVE, ybuf[:, c, g * 512:(g + 1) * 512].rearrange("p (h e) -> p h e", e=64),
                       bank(g).rearrange("p (h e) -> p h e", e=64),
                       bcl(dta["cs"][:, c, col0 + g * 8:col0 + g * 8 + 8], 64), ALU.mult, [pk(g), "cs"], [(ykey, c)])
            x2 = xw[c % 2]
            tt(DVE, x2[:].rearrange("p (h e) -> p h e", e=64), xs_tok[:, c, :].rearrange("p (h e) -> p h e", e=64),
               bcl(dta["wts"][:, c, col0:col0 + 16], 64), ALU.mult, [("xs", c), "wts"], [("xw", c % 2)])
            for g in range(2):
                mm(bank(2 + g), B_tok[:, c, g * 128:(g + 1) * 128], x2[:, g * 512:(g + 1) * 512], True, True,
                   ["Btok", ("xw", c % 2)], [pk(2 + g)])
            tt(DVE, tmpf[:].rearrange("p (h e) -> p h e", e=64), S[:].rearrange("p (h e) -> p h e", e=64),
               bcl(dta["tot"][:, c, col0:col0 + 16], 64), ALU.mult, ["S", "tot"], ["tmpS"])
            for g in range(2):
                tt(DVE, S[:, g * 512:(g + 1) * 512], tmpf[:, g * 512:(g + 1) * 512], bank(2 + g), ALU.add,
                   ["tmpS", pk(2 + g)], ["S"])
            cp(ACT, Sbf[:], S[:], ["S"], ["Sbf"])

    xbcT_t, _ = sbt("xbcT", [128, 12, T + 4], BF16, RA)
    RBt, _ = sbt("RB", [128, 8 * T], BF16, RB)
    hlT = RBt[:].rearrange("p (k t) -> p k t", k=8)
    xs_tok = RBt[:].rearrange("p (i f) -> p i f", f=D)
    SA = Alloc(ST)
    Hs = SA("Hs", [128, D]); Gs = SA("Gs", [128, D])
    Hbf = SA("Hbf", [128, D], BF16); Gbf = SA("Gbf", [128, D], BF16)
    dtraw = SA("dtraw", [128, NT, 32])
    assert SA.off <= WB

    for b in range(NB):
        A1 = Alloc(WB + 8 * NA * 2)
        if b > 0:
            dma(POOL, wA[:], w_in.rearrange("(k p) n -> p k n", p=128)[:, :, X0:X0 + NA], [], ["wA"])
        markW = A1.off
        hlT_c = A1("hlT_c", [128, 8, TC], BF16)
        xbcT_c = A1("xbcT_c", [128, 12, TC + 4], BF16)
        xs_c = A1("xs_c", [128, NTC, D], BF16)
        B_c = A1("B_c", [128, NTC, 256], BF16)
        dtraw_c = A1("dtraw_c", [128, NTC, 32])
        dta_c = {n: A1("c_" + n, [128, NTC, 32]) for n in ["dt", "lndt", "a", "cs", "tot", "wts"]}
        xseg = A1("xseg", [128, 4, D]); xn = A1("xn", [128, 4, D], BF16)
        junk = A1("junk", [128, D], BF16); ss = A1("ss", [128, 16])
        dg = A1("dg", [128, 60 * 128], BF16)
        xsh_c = A1("xsh_c", [128, 12, TC + 4], BF16)
        xw = [A1("xw%d" % i, [128, D], BF16) for i in range(2)]
        tmpf = A1("tmpf", [128, D])
        ms(POOL, xbcT_c[:, :, 0:2], 0.0, [("xbc", j) for j in range(12)])
        ms(POOL, xbcT_c[:, :, TC + 2:TC + 4], 0.0, [("xbc", j) for j in range(12)])
        norm_to_T(ctx[b], TC, scale1[NB], modT[NB], [("scale1", NB), ("modT", NB)], hlT_c, "hlT", xseg, xn, junk, ss)
        sweepA(hlT_c, TC, wA, xbcT_c, dtraw_c)
        dt_stage(TC, dtraw_c, dta_c)
        conv_stage(TC, xbcT_c, dg, xsh_c, xs_c, B_c)
        ms(DVE, Hs[:], 0.0, ["S"]); ms(DVE, Gs[:], 0.0, ["S"])
        state_pass(NTC, range(NTC), 0, xs_c, B_c, xbcT_c, dta_c, Hs, Hbf, xw, tmpf)
        P.barrier()
        state_pass(NTC, range(NTC - 1, -1, -1), 16, xs_c, B_c, xbcT_c, dta_c, Gs, Gbf, xw, tmpf)
        P.barrier()

        A2 = Alloc(markW)
        wb1 = A2("wb1", [128, 8, 2048], BF16)
        mark = A2.off
        xseg = [A2("xseg%d" % i, [128, 4, D]) for i in range(2)]
        xn = A2("xn", [128, 4, D], BF16)
        junk = A2("junk", [128, D], BF16); ss = [A2("ss%d" % i, [128, 16]) for i in range(2)]
        dma(POOL, wb1[:], w_in.rearrange("(k p) n -> p k n", p=128)[:, :, 0:2048], [], ["wb1"])
        norm_to_T(x[b], T, scale1[b], modT[b], [("scale1", b), ("modT", b)], hlT, "hlT", xseg, xn, junk, ss)
        P.barrier()

        A3 = Alloc(mark)
        gnw_bc = A3("gnw_bc", [128, D]); bs_bc = A3("bs_bc", [128, D])
        dma(SP, gnw_bc[:], prow(gnw), [], ["gnw_bc"]); dma(SP, bs_bc[:], prow(bsv), [], ["bs_bc"])
        uT = A3("uT", [128, 8, 512], BF16); vv = A3("vv", [128, 4, D], BF16)
        sq = A3("sq", [128, D], BF16); vnf = A3("vnf", [128, D]); vn2 = A3("vn2", [128, D], BF16)
        mxt = A3("mxt", [128, D]); gmT = [A3("gmT%d" % i, [128, D], BF16) for i in range(2)]
        zt = [A3("zt%d" % i, [128, D], BF16) for i in range(2)]
        tz = A3("tz", [128, 512]); ssv = A3("ssv", [128, 96])
        rr = 0
        for s0 in range(0, T, 512):
            for h in range(8):
                bk = 4 + rr % 4; rr += 1
                for k in range(8):
                    mm(bank(bk), wb1[:, k, h * 128:(h + 1) * 128], hlT[:, k, s0:s0 + 512], k == 0, k == 7, ["wb1", "hlT"], [pk(bk)])
                act(uT[:, h, :], bank(bk), AF.Gelu, [pk(bk)], ["uT"])
            for i in range(4):
                t0 = s0 + i * 128
                for hf in range(2):
                    bk = 4 + rr % 4; rr += 1
                    for k in range(8):
                        mm(bank(bk), hlT[:, k, t0:t0 + 128], wb1[:, k, 1024 + hf * 512:1024 + (hf + 1) * 512], k == 0, k == 7,
                           ["wb1", "hlT"], [pk(bk)])
                    act(vv[:, i, hf * 512:(hf + 1) * 512], bank(bk), AF.Gelu, [pk(bk)], [("vv", i)])
                tt(DVE, sq[:], vv[:, i, :], vv[:, i, :], ALU.mult, [("vv", i)], ["sq"])
                P.add(DVE, (lambda o, a: (lambda h_: h_.tensor_reduce(out=o, in_=a, axis=AX.X, op=ALU.add)))(
                    ssv[:, i * 8:(i + 1) * 8], sq[:].rearrange("p (h e) -> p h e", e=128)), ["sq"], ["ssv"])
            act(ssv[:, 32:64], ssv[:, 0:32], AF.Ln, ["ssv", "eps1"], ["ssv"], bias=eps1[:], scale=1.0 / 128)
            act(ssv[:, 64:96], ssv[:, 32:64], AF.Exp, ["ssv"], ["rsv"], scale=-0.5)
            for i in range(4):
                t0 = s0 + i * 128
                tt(DVE, vnf[:].rearrange("p (h e) -> p h e", e=128), vv[:, i, :].rearrange("p (h e) -> p h e", e=128),
                   bcl(ssv[:, 64 + i * 8:64 + (i + 1) * 8], 128), ALU.mult, [("vv", i), "rsv"], ["vnf"])
                tt(DVE, vn2[:], vnf[:], gnw_bc[:], ALU.mult, ["vnf", "gnw_bc"], ["vn2"])
                for h in range(8):
                    mm(bank(h // 4)[:, (h % 4) * 128:(h % 4 + 1) * 128], vn2[:, h * 128:(h + 1) * 128], WsT[:, h, :], True, True,
                       ["vn2", "WsT"], [pk(h // 4)])
                g_ = gmT[i % 2]
                for hf in range(2):
                    tt(DVE, mxt[:, hf * 512:(hf + 1) * 512], bank(hf), bs_bc[:, hf * 512:(hf + 1) * 512], ALU.add,
                       [pk(hf), "bs_bc"], ["mxt"])
                tt(DVE, g_[:].rearrange("p (h e) -> p h e", e=128), mxt[:].rearrange("p (h e) -> p h e", e=128),
                   uT[:, :, i * 128:(i + 1) * 128], ALU.mult, ["mxt", "uT"], [("gmT", i % 2)])
                dma(POOL, gm_s[t0 // 128], g_[:], [("gmT", i % 2)], [("gm_s", t0 // 128)])
        ms(POOL, xbcT_t[:, :, 0:2], 0.0, [("xbc", j) for j in range(12)])
        ms(POOL, xbcT_t[:, :, T + 2:T + 4], 0.0, [("xbc", j) for j in range(12)])
        sweepA(hlT, T, wA, xbcT_t, dtraw)
        dma(POOL, wb1[:, :, 0:1024], w_in.rearrange("(k p) n -> p k n", p=128)[:, :, Z0:Z0 + 1024], [], ["wb1"])
        for i in range(NT):
            t0 = i * 128
            z_ = zt[i % 2]
            for hf in range(2):
                bk = 4 + rr % 4; rr += 1
                for k in range(8):
                    mm(bank(bk), hlT[:, k, t0:t0 + 128], wb1[:, k, hf * 512:(hf + 1) * 512], k == 0, k == 7, ["wb1", "hlT"], [pk(bk)])
                act(tz[:], bank(bk), AF.Tanh, [pk(bk)], ["tz"], scale=0.5)
                stt(DVE, z_[:, hf * 512:(hf + 1) * 512], tz[:], 1.0, bank(bk), ALU.add, ALU.mult, ["tz", pk(bk)], [("zt", i % 2)])
            dma(POOL, z_s[t0:t0 + 128, :], z_[:], [("zt", i % 2)], [("z_s", i)])
        P.barrier()

        A4 = Alloc(WB)
        B_tok = A4("B_tok", [128, NT, 256], BF16)
        dta = {n: A4("l_" + n, [128, NT, 32]) for n in ["dt", "lndt", "a", "cs", "tot", "wts"]}
        dl = A4("dl", [128, NT, 16]); coef = A4("coef", [128, NT, 16]); cbd = A4("cbd", [128, NT, 2])
        mark5 = A4.off
        dg = A4("dg", [128, 60 * 128], BF16)
        xsh = A4("xsh", [128, 12, T + 4], BF16)
        dt_stage(T, dtraw, dta)
        conv_stage(T, xbcT_t, dg, xsh, xs_tok, B_tok)
        tt(DVE, dl[:], dta["lndt"][:, :, 16:32], dta["lndt"][:, :, 0:16], ALU.subtract, ["lndt"], ["dl"])
        P.barrier()

        A5 = Alloc(mark5)
        xw = [A5("xw%d" % i, [128, D], BF16) for i in range(2)]
        tmpf = A5("tmpf", [128, D])
        ybt, _ = sbt("ybuf", [128, NT, D], BF16, RA)
        state_pass(NT, range(NT - 1, -1, -1), 16, xs_tok, B_tok, xbcT_t, dta, Gs, Gbf, xw, tmpf, ybuf=ybt, ykey="yb")
        P.barrier()

        rf_ = A5("rf", [128, 16 * 128], BF16); rb_ = A5("rb", [128, 16 * 128], BF16)
        rf = [rf_, rf_]; rb = [rb_, rb_]
        wo, _ = sbt("wo", [128, 16, D], BF16, 180032)
        dma(POOL, wo[:], w_out.rearrange("(k p) n -> p k n", p=128), [], ["wo"])
        Mf = [A5("Mf%d" % i, [128, 16 * 128], BF16) for i in range(2)]
        Mb = [A5("Mb%d" % i, [128, 16 * 128], BF16) for i in range(2)]
        E4_ = A5("E4", [128, 512], BF16); E4 = [E4_, E4_]
        t1b = [A5("t1b%d" % i, [128, D], BF16) for i in range(2)]
        xdf = [A5("xdf%d" % i, [128, D], BF16) for i in range(2)]
        xdb = [A5("xdb%d" % i, [128, D], BF16) for i in range(2)]
        xc = xw[1]
        cbf = A5("cbf", [128, 256], BF16); cbb = A5("cbb", [128, 256], BF16); junkf = A5("junkf", [128, 128])
        t1 = A5("t1", [128, D], BF16); sz_ = A5("sz", [128, D], BF16); szt = [sz_, sz_]
        gn = xw[1]; ssg = A5("ssg", [128, 8]); junk = xc
        assert A5.off <= 180032, A5.off

        def v64(a):
            return a.rearrange("p (h e) -> p h e", e=64)

        def A_yoff(c):
            tsl = slice(2 + c * 128, 2 + (c + 1) * 128)
            for g in range(2):
                mm(bank(g), xbcT_t[:, 10 + g, tsl], Hbf[:, g * 512:(g + 1) * 512], True, True, [("xbc", 10 + g), "Sbf"], [pk(g)])

        def A_dve_nodep(c):
            p = c % 2
            tt(DVE, v64(xw[0][:]), v64(xs_tok[:, c, :]), bcl(dta["wts"][:, c, 0:16], 64), ALU.mult, [("xs", c), "wts"], [("xw", 0)])
            tt(DVE, v64(tmpf[:]), v64(Hs[:]), bcl(dta["tot"][:, c, 0:16], 64), ALU.mult, ["S", "tot"], ["tmpS"])
            tt(DVE, v64(xdf[p][:]), v64(xs_tok[:, c, :]), bcl(dta["dt"][:, c, 0:16], 64), ALU.mult, [("xs", c), "dt"], [("xdf", p)])
            tt(DVE, v64(xdb[p][:]), v64(xs_tok[:, c, :]), bcl(dta["dt"][:, c, 16:32], 64), ALU.mult, [("xs", c), "dt"], [("xdb", p)])

        def A_pe2(c):
            tsl = slice(2 + c * 128, 2 + (c + 1) * 128)
            for g in range(2):
                mm(bank(3 + g), B_tok[:, c, g * 128:(g + 1) * 128], xw[0][:, g * 512:(g + 1) * 512], True, True,
                   ["Btok", ("xw", 0)], [pk(3 + g)])
            for g in range(2):
                mm(bank(2)[:, g * 128:(g + 1) * 128], xbcT_t[:, 8 + g, tsl], xbcT_t[:, 10 + g, tsl], True, True,
                   [("xbc", 8 + g), ("xbc", 10 + g)], [pk(2)])

        def A_dve_dep(c):
            p = c % 2
            for g in range(2):
                tt(DVE, v64(t1b[p][:, g * 512:(g + 1) * 512]), v64(bank(g)),
                   bcl(dta["cs"][:, c, g * 8:g * 8 + 8], 64), ALU.mult, [pk(g), "cs"], [("t1b", p)])
            for g in range(2):
                tt(DVE, Hs[:, g * 512:(g + 1) * 512], tmpf[:, g * 512:(g + 1) * 512], bank(3 + g), ALU.add, ["tmpS", pk(3 + g)], ["S"])
            cp(ACT, Hbf[:], Hs[:], ["S"], ["Sbf"])
            c3 = bank(2)[:, 0:256].rearrange("p (g t) -> p g t", t=128)
            tt(DVE, cbf[:].rearrange("p (g t) -> p g t", t=128), c3, bcm(Umat[:], 2), ALU.mult, [pk(2), "Umat"], ["cbf"])
            tt(DVE, cbb[:].rearrange("p (g t) -> p g t", t=128), c3, bcm(mLs[:], 2), ALU.mult, [pk(2), "mLs"], ["cbb"])
            for g in range(2):
                tt(DVE, junkf[:], bank(2)[:, g * 128:(g + 1) * 128], identf[:], ALU.mult, [pk(2), "identf"], ["junkf"])
                P.add(DVE, (lambda o, a_: (lambda h_: h_.tensor_reduce(out=o, in_=a_, axis=AX.X, op=ALU.add)))(
                    cbd[:, c, g:g + 1], junkf[:]), ["junkf"], [("cbd", c)])

        def A_builds(c):
            p = c % 2
            for h in range(16):
                act(rf[p][:, h * 128:(h + 1) * 128], Ubf[:], AF.Identity, ["Ubf", "a"], [("rf", 0, h // 4)], scale=dta["a"][:, c, h:h + 1])
                act(rb[p][:, h * 128:(h + 1) * 128], Lbf[:], AF.Identity, ["Lbf", "a"], [("rb", 0, h // 4)], scale=dta["a"][:, c, 16 + h:17 + h])

        def A_decay(c):
            p = c % 2
            for q in range(4):
                bk = 3 + q % 2
                mm(bank(bk), SLf[:], rf[p][:, q * 512:(q + 1) * 512], True, False, ["SLf", ("rf", 0, q)], [pk(bk)])
                mm(bank(bk), SLb[:], rb[p][:, q * 512:(q + 1) * 512], False, True, ["SLb", ("rb", 0, q)], [pk(bk)])
                act(E4[q % 2][:], bank(bk), AF.Exp, [pk(bk)], [("E4", 0)])
                g = q // 2
                tt(DVE, Mf[p][:, q * 512:(q + 1) * 512].rearrange("p (h t) -> p h t", t=128),
                   E4[q % 2][:].rearrange("p (h t) -> p h t", t=128), bcm(cbf[:, g * 128:(g + 1) * 128], 4), ALU.mult,
                   [("E4", 0), "cbf"], [("Mf", p, q)])
                tt(DVE, Mb[p][:, q * 512:(q + 1) * 512].rearrange("p (h t) -> p h t", t=128),
                   E4[q % 2][:].rearrange("p (h t) -> p h t", t=128), bcm(cbb[:, g * 128:(g + 1) * 128], 4), ALU.mult,
                   [("E4", 0), "cbb"], [("Mb", p, q)])

        def B_front(c):
            p = c % 2
            for g in range(2):
                stt(DVE, coef[:, c, g * 8:(g + 1) * 8], dta["dt"][:, c, 16 + g * 8:16 + (g + 1) * 8], cbd[:, c, g:g + 1],
                    dD_t[:, g * 8:(g + 1) * 8], ALU.mult, ALU.add, ["dt", ("cbd", c), "dD"], [("coef", c)])
            tt(DVE, v64(xc[:]), v64(xs_tok[:, c, :]), bcl(coef[:, c, :], 64), ALU.mult, [("xs", c), ("coef", c)], [("xw", 1)])
            for hb in range(2):
                bk = 5 + hb
                cs_ = slice(hb * 512, (hb + 1) * 512)
                mm(bank(bk), ident[:], t1b[p][:, cs_], True, False, ["ident", ("t1b", p)], [pk(bk)])
                mm(bank(bk), ident[:], ybt[:, c, cs_], False, False, ["ident", ("yb", c)], [pk(bk)])
                mm(bank(bk), ident[:], xc[:, cs_], False, False, ["ident", ("xw", 1)], [pk(bk)])
                for hh in range(8):
                    H = hb * 8 + hh
                    o = bank(bk)[:, hh * 64:(hh + 1) * 64]
                    mm(o, Mf[p][:, H * 128:(H + 1) * 128], xdf[p][:, H * 64:(H + 1) * 64], False, False,
                       [("Mf", p, H // 4), ("xdf", p)], [pk(bk)])
                    mm(o, Mb[p][:, H * 128:(H + 1) * 128], xdb[p][:, H * 64:(H + 1) * 64], False, hh == 7,
                       [("Mb", p, H // 4), ("xdb", p)], [pk(bk)])

        def B_gate(c):
            for hb in range(2):
                cs_ = slice(hb * 512, (hb + 1) * 512)
                tt(DVE, t1[:, cs_], bank(5 + hb), szt[0][:, cs_], ALU.mult, [pk(5 + hb), "sz"], ["t1g"])

        def B_norm(c):
            ms(DVE, ssg[:, 0:2], 0.0, ["ssg"])
            for g in range(2):
                act(junk[:, 0:512], t1[:, g * 512:(g + 1) * 512], AF.Square, ["t1g"], [("xw", 1), "ssg"], accum=ssg[:, g:g + 1])
            act(ssg[:, 2:4], ssg[:, 0:2], AF.Ln, ["ssg", "eps4"], ["ssg"], bias=eps4[:], scale=1.0 / 512)
            act(ssg[:, 4:6], ssg[:, 2:4], AF.Exp, ["ssg"], ["rsg"], scale=-0.5)
            for g in range(2):
                act(gn[:, g * 512:(g + 1) * 512], t1[:, g * 512:(g + 1) * 512], AF.Identity, ["t1g", "rsg"], [("xw", 1)],
                    scale=ssg[:, 4 + g:5 + g])

        def B_out(c):
            pv = bankb(7)
            for k in range(8):
                tr(pv[:, k * 128:(k + 1) * 128], gn[:, k * 128:(k + 1) * 128], [("xw", 1)], [pk(7)])
            cp(ACT, ybt[:, c, :], pv, [pk(7)], [("yb", c)])
            if c + 1 < NT:
                dma(SP, sz_[:], z_s[(c + 1) * 128:(c + 2) * 128, :], [("z_s", c + 1)], ["sz"])

        dma(SP, sz_[:], z_s[0:128, :], [("z_s", 0)], ["sz"])
        A_builds(0); A_yoff(0); A_dve_nodep(0); A_pe2(0); A_dve_dep(0); A_decay(0)
        for c in range(NT):
            nx = c + 1 < NT
            if nx:
                A_builds(c + 1)
            B_front(c)
            if nx:
                A_yoff(c + 1)
                A_dve_nodep(c + 1)
                A_pe2(c + 1)
            B_gate(c)
            B_norm(c)
            if nx:
                A_dve_dep(c + 1)
            B_out(c)
            if nx:
                A_decay(c + 1)
        P.barrier()

        A7 = Alloc(155456)
        w13t, _ = sbt("w13t", [128, 8, 5632], BF16, 41984)
        g1_bc = A7("g1_bc", [128, D])
        xt7 = [A7("xt7_%d" % i, [128, D]) for i in range(2)]
        gmt7 = [A7("gmt7_%d" % i, [128, D], BF16) for i in range(2)]
        x1t = [A7("x1t_%d" % i, [128, D]) for i in range(2)]
        assert A7.off <= 180032, A7.off
        dma(POOL, w13t[:], w13.rearrange("(k p) n -> p k n", p=128), [], ["w13"])
        dma(SP, g1_bc[:], prow(m_s[b:b + 1, 2 * D:3 * D]), ["m_s"], ["g1_bc"])
        for k in range(16):
            if k >= 8:
                ts(DVE, wo[:, k, :], wo[:, k, :], snw_t[:, k - 8:k - 7], None, ALU.mult, None, ["wo", "snw"], [("wok", k)])
            if k < 8:
                pass
        for c in range(NT):
            t0 = c * 128
            dma(SP, xt7[c % 2][:], x[b, t0:t0 + 128, :], [], [("xt7", c % 2)])
            dma(SP, gmt7[c % 2][:], gm_s[c], [("gm_s", c)], [("gmt7", c % 2)])
            for hf in range(2):
                bk = (c % 2) * 2 + hf
                for k in range(16):
                    lhs = gmt7[c % 2][:, k * 128:(k + 1) * 128] if k < 8 else ybt[:, c, (k - 8) * 128:(k - 7) * 128]
                    mm(bank(bk), lhs, wo[:, k, hf * 512:(hf + 1) * 512], k == 0, k == 15,
                       [("gmt7", c % 2), ("yb", c), "wo" if k < 8 else ("wok", k)], [pk(bk)])
                tt(DVE, x1t[c % 2][:, hf * 512:(hf + 1) * 512], bank(bk), g1_bc[:, hf * 512:(hf + 1) * 512], ALU.mult,
                   [pk(bk), "g1_bc"], [("x1t", c % 2)])
                tt(DVE, x1t[c % 2][:, hf * 512:(hf + 1) * 512], x1t[c % 2][:, hf * 512:(hf + 1) * 512],
                   xt7[c % 2][:, hf * 512:(hf + 1) * 512], ALU.add, [("xt7", c % 2), ("x1t", c % 2)], [("x1t", c % 2)])
            dma(POOL, x1_s[t0:t0 + 128, :], x1t[c % 2][:], [("x1t", c % 2)], [("x1_s", c)])
        P.barrier()

        import os as _os
        if "8" in _os.environ.get("KSKIP", ""):
            continue
        A8 = Alloc(RA)
        xseg = A8("xseg", [128, 4, D]); xn = A8("xn", [128, 4, D], BF16)
        hl2T = A8("hl2T", [128, 8, 512], BF16)
        assert A8.off <= 41984, A8.off
        w2t, _ = sbt("w2t", [128, 22, D], BF16, 132096)
        A8 = Alloc(177152)
        hT = A8("hT", [128, 22, 512], BF16)
        sg = A8("sg", [128, 512]); fnw_bc = A8("fnw_bc", [128, D]); yo = A8("yo", [128, D])
        junk = A8("junk", [128, D], BF16); ss = A8("ss", [128, 16]); ss3 = A8("ss3", [128, 16])
        g2t = yo
        dma(POOL, w2t[:], w2.rearrange("(k p) n -> p k n", p=128), [], ["w2"])
        dma(SP, g2t[:], prow(m_s[b:b + 1, 5 * D:6 * D]), ["m_s"], ["g2_bc"])
        dma(SP, fnw_bc[:], prow(fnw), [], ["fnw_bc"])
        for s0 in range(0, T, 512):
            norm_to_T(x1_s[s0:s0 + 512, :], 512, scale2[b], modT[b][:, 24:32], [("scale2", b), ("modT", b)], hl2T, "hl2T",
                      xseg, xn, junk, ss)
            for j in range(22):
                bg, bu = 4 + (j % 2) * 2, 5 + (j % 2) * 2
                for k in range(8):
                    mm(bank(bg), w13t[:, k, j * 128:(j + 1) * 128], hl2T[:, k, :], k == 0, k == 7, ["w13", "hl2T"], [pk(bg)])
                for k in range(8):
                    mm(bank(bu), w13t[:, k, 2816 + j * 128:2816 + (j + 1) * 128], hl2T[:, k, :], k == 0, k == 7, ["w13", "hl2T"], [pk(bu)])
                act(sg[:], bank(bg), AF.Silu, [pk(bg)], ["sg"])
                tt(DVE, hT[:, j, :], sg[:], bank(bu), ALU.mult, ["sg", pk(bu)], [("hT", j)])
            if s0 == 0:
                for k in range(22):
                    tt(DVE if k % 2 == 0 else POOL, w2t[:, k, :], w2t[:, k, :], g2t[:], ALU.mult,
                       ["w2", ("w2k", k), "g2_bc"], [("w2k", k), "yo"])
            ms(DVE, ss3[:, 0:8], 0.0, ["ss3"])
            for i in range(4):
                for hf in range(2):
                    bk = hf
                    for j in range(22):
                        mm(bank(bk), hT[:, j, i * 128:(i + 1) * 128], w2t[:, j, hf * 512:(hf + 1) * 512], j == 0, j == 21,
                           [("hT", j), ("w2k", j)], [pk(bk)])
                    tt(DVE, xseg[:, i, hf * 512:(hf + 1) * 512], xseg[:, i, hf * 512:(hf + 1) * 512], bank(bk), ALU.add,
                       [("xseg", 0, i), pk(bk)], [("xseg", 0, i)])
                act(junk[:], xseg[:, i, :], AF.Square, [("xseg", 0, i)], ["junk", "ss3"], accum=ss3[:, i:i + 1])
            act(ss3[:, 4:8], ss3[:, 0:4], AF.Ln, ["ss3", "eps1"], ["ss3"], bias=eps1[:], scale=1.0 / D)
            act(ss3[:, 8:12], ss3[:, 4:8], AF.Exp, ["ss3"], ["rs3"], scale=-0.5)
            for i in range(4):
                stt(DVE, yo[:], xseg[:, i, :], ss3[:, 8 + i:9 + i], fnw_bc[:], ALU.mult, ALU.mult, [("xseg", 0, i), "rs3", "fnw_bc"], ["yo"])
                dma(POOL, out[b, s0 + i * 128:s0 + (i + 1) * 128, :], yo[:], ["yo"], [("out", s0 // 128 + i)])
        P.barrier()

    run_prog(nc, P)
    return nc


def _prep_common(inp):
    f = np.float32
    g = lambda k: np.asarray(inp[k], dtype=f)
    fm = lambda v: np.ascontiguousarray(v.reshape(-1, 128).T)
    cm = {
        "w_ada": np.ascontiguousarray(g("w_ada")[0]), "b_ada": np.ascontiguousarray(g("b_ada")[0].reshape(1, -1)),
        "w_in": np.ascontiguousarray(g("w_in")[0]), "w_out": np.ascontiguousarray(g("w_out")[0]),
        "w13": np.ascontiguousarray(g("ffn_w13")[0]), "w2": np.ascontiguousarray(g("ffn_w2")[0]),
        "n1w": fm(g("norm1_w")[0]), "n2w": fm(g("norm2_w")[0]), "snw": fm(g("ssd_norm_w")[0]),
        "gnw": np.ascontiguousarray(g("gm_norm_w")[0].reshape(1, -1)),
        "bsv": np.ascontiguousarray(g("gm_bs")[0].reshape(1, -1)),
        "fnw": np.ascontiguousarray(g("final_norm_w").reshape(1, -1)),
        "wsT": np.ascontiguousarray(g("gm_ws")[0].transpose(2, 0, 1)),
        "cw": np.ascontiguousarray(g("conv_w")[0].reshape(5, 12, 128).transpose(2, 1, 0)),
        "cbv": fm(g("conv_b")[0]),
        "alog": np.ascontiguousarray(g("ssd_A_log")[0].reshape(1, 32)),
        "dtbias": np.ascontiguousarray(g("ssd_dt_bias")[0].reshape(1, 32)),
        "dD": np.ascontiguousarray(g("ssd_D")[0].reshape(1, 16)),
    }
    return cm


_NC_CACHE = {}


def run_cores(inp, ncores, NB):
    x = np.asarray(inp["x"], dtype=np.float32)
    ctx = np.asarray(inp["ctx"], dtype=np.float32)
    c = np.asarray(inp["c"], dtype=np.float32)
    c_ctx = np.asarray(inp["c_ctx"], dtype=np.float32)
    T, TC = x.shape[1], ctx.shape[1]
    key = (T, TC, NB)
    if key not in _NC_CACHE:
        _NC_CACHE[key] = build(T, TC, NB)
    nc = _NC_CACHE[key]
    cm = _prep_common(inp)
    maps = []
    for i in range(ncores):
        sl = slice(i * NB, (i + 1) * NB)
        vecs = np.concatenate([c[sl], c_ctx[None, :]], axis=0)
        cT = np.ascontiguousarray(vecs.reshape(NB + 1, 8, 128).transpose(2, 1, 0))
        m = dict(cm)
        m["x"] = np.ascontiguousarray(x[sl]); m["ctx"] = np.ascontiguousarray(ctx[sl]); m["cT"] = cT
        maps.append(m)
    res = run_bass_kernel_spmd(nc, maps, core_ids=list(range(ncores)))
    return np.concatenate([np.asarray(r["out"]) for r in res.results], axis=0).astype(np.float32)


def kernel(**inputs):
    return run_cores(inputs, 8, 2)
```

```python
from concourse.bass_utils import run_bass_kernel_spmd
import numpy as np
import concourse.bass as bass
import concourse.mybir as mybir

F32 = mybir.dt.float32
BF16 = mybir.dt.bfloat16
AF = mybir.ActivationFunctionType
ALU = mybir.AluOpType
AX = mybir.AxisListType

PE, ACT, DVE, POOL, SP = "pe", "act", "dve", "pool", "sp"
ENGS = [PE, ACT, DVE, POOL, SP]
N_DMA_SEMS = 8


class Op:
    __slots__ = ("eng", "fn", "reads", "writes", "dma", "idx", "deps", "signal",
                 "sigval", "dsem", "dval", "guard")

    def __init__(self, eng, fn, reads, writes, dma):
        self.eng, self.fn, self.reads, self.writes, self.dma = eng, fn, reads, writes, dma
        self.deps = []
        self.signal = False
        self.sigval = 0
        self.dsem = None
        self.dval = 0
        self.guard = None


class Prog:
    def __init__(self, nc):
        self.nc = nc
        self.ops = []
        self.last_w = {}
        self.readers = {}
        self.dma_rr = {e: 0 for e in ENGS}
        self.dma_last = {}
        self.last_eng = {}
        self.bar = {}

    def add(self, eng, fn, reads=(), writes=(), dma=False):
        reads = list(reads)
        writes = list(writes)
        ps_r = [k for k in reads if isinstance(k, str) and k.startswith("ps")]
        if ps_r:
            reads = [k for k in reads if k not in ps_r]
            writes = writes + [k for k in ps_r if k not in writes]
        op = Op(eng, fn, reads, writes, dma)
        op.idx = len(self.ops)
        deps = {}

        def dep(p, kind):
            if p is None:
                return
            if p.dma or dma:
                deps[p.idx] = p
                return
            if p.eng == eng:
                if eng == PE:
                    return
                deps[p.idx] = p
                return
            deps[p.idx] = p

        for k in reads:
            dep(self.last_w.get(k), "raw")
        for k in writes:
            dep(self.last_w.get(k), "waw")
            for r in self.readers.get(k, ()):
                dep(r, "war")
        for k in reads:
            lst = self.readers.setdefault(k, [])
            if not dma and eng != POOL:
                lst[:] = [r for r in lst if r.dma or r.eng != eng]
            lst.append(op)
        for k in writes:
            self.last_w[k] = op
            self.readers[k] = []
        for p in self.bar.pop(eng, ()):
            if p.dma or dma or p.eng != eng:
                deps[p.idx] = p
        op.deps = list(deps.values())
        if not dma:
            self.last_eng[eng] = op
        if dma:
            slot = self.dma_rr[eng] % N_DMA_SEMS
            self.dma_rr[eng] += 1
            op.dsem = (eng, slot)
            prev = self.dma_last.get(op.dsem)
            op.dval = (prev.dval if prev else 0) + 16
            op.guard = prev
            self.dma_last[op.dsem] = op
        self.ops.append(op)
        return op

    def barrier(self):
        pend = list(self.last_eng.values()) + list(self.dma_last.values())
        for e in ENGS:
            self.bar[e] = list(pend)

    def emit(self, sems, dsems):
        nc = self.nc
        for op in self.ops:
            for p in op.deps:
                if not p.dma:
                    p.signal = True
        cnt = {e: 0 for e in ENGS}
        for op in self.ops:
            if op.signal and not op.dma:
                cnt[op.eng] += 1
                op.sigval = cnt[op.eng]
        per = {e: [o for o in self.ops if o.eng == e] for e in ENGS}
        handles = {PE: nc.tensor, ACT: nc.scalar, DVE: nc.vector, POOL: nc.gpsimd, SP: nc.sync}
        self.n_waits = 0

        def run(eng, h=None):
            if h is None:
                h = handles[eng]
            waited = {}
            for op in per[eng]:
                need = {}
                for p in op.deps:
                    if p.dma:
                        key, val = ("d",) + p.dsem, p.dval
                    else:
                        key, val = ("e", p.eng), p.sigval
                    if need.get(key, 0) < val:
                        need[key] = val
                if op.guard is not None:
                    key = ("d",) + op.dsem
                    if need.get(key, 0) < op.guard.dval:
                        need[key] = op.guard.dval
                todo = []
                for key, val in need.items():
                    if waited.get(key, 0) >= val:
                        continue
                    waited[key] = val
                    todo.append((dsems[key[1:]] if key[0] == "d" else sems[key[1]], val))
                for s, val in todo[:-1]:
                    h.wait_ge(s, val)
                    self.n_waits += 1
                ins = op.fn(h)
                if todo:
                    ins._wait_ge(todo[-1][0], todo[-1][1])
                if op.dma:
                    ins.then_inc(dsems[op.dsem], 16)
                elif op.signal:
                    ins.then_inc(sems[eng], 1)
            for (e, slot), last in self.dma_last.items():
                if e == eng:
                    h.wait_ge(dsems[(e, slot)], last.dval)

        return run


def run_prog(nc, prog):
    from contextlib import ExitStack
    with ExitStack() as st:
        sems = {e: st.enter_context(nc.semaphore("s_" + e)) for e in ENGS}
        dsems = {}
        for e in ENGS:
            if prog.dma_rr[e] > 0:
                for i in range(N_DMA_SEMS):
                    dsems[(e, i)] = st.enter_context(nc.semaphore("d_%s%d" % (e, i)))
        run = prog.emit(sems, dsems)
        block = st.enter_context(nc.Block())

        @block.tensor
        def _(e):
            run(PE, e)

        @block.scalar
        def _(e):
            run(ACT, e)

        @block.vector
        def _(e):
            run(DVE, e)

        @block.gpsimd
        def _(e):
            run(POOL, e)

        @block.sync
        def _(e):
            run(SP, e)
D = 1024
EPS = 1e-6
U0, V0, Z0, X0, DT0 = 0, 1024, 2048, 3072, 4608
NA = 1568


def bcl(a, n):
    return bass.AP(a.tensor, a.offset, list(a.ap) + [[0, n]])


def bcm(a, n):
    return bass.AP(a.tensor, a.offset, [a.ap[0], [0, n]] + list(a.ap[1:]))


def prow(a, n=128):
    return bass.AP(a.tensor, a.offset, [[0, n]] + list(a.ap[1:]))


def build(T, TC, NB):
    nc = bass.Bass("TRN2", target_bir_lowering=False)
    NT, NTC = T // 128, TC // 128
    NJ = NB + 1

    def din(name, shape, dt=F32):
        return nc.dram_tensor(name, shape, dt, kind="ExternalInput").ap()

    x = din("x", [NB, T, D]); ctx = din("ctx", [NB, TC, D]); cT = din("cT", [128, 8, NJ])
    w_ada = din("w_ada", [D, 6 * D]); b_ada = din("b_ada", [1, 6 * D])
    w_in = din("w_in", [D, 4640]); w_out = din("w_out", [2 * D, D])
    w13 = din("w13", [D, 5632]); w2 = din("w2", [2816, D])
    n1w = din("n1w", [128, 8]); n2w = din("n2w", [128, 8]); snw = din("snw", [128, 8])
    gnw = din("gnw", [1, D]); bsv = din("bsv", [1, D]); fnw = din("fnw", [1, D])
    wsT = din("wsT", [128, 8, 128]); cw = din("cw", [128, 12, 5]); cbv = din("cbv", [128, 12])
    alog = din("alog", [1, 32]); dtbias = din("dtbias", [1, 32]); dD = din("dD", [1, 16])
    out = nc.dram_tensor("out", [NB, T, D], F32, kind="ExternalOutput").ap()
    m_s = nc.dram_tensor("m_s", [NJ, 6 * D], F32).ap()
    z_s = nc.dram_tensor("z_s", [T, D], BF16).ap()
    gm_s = nc.dram_tensor("gm_s", [T // 128, 128, D], BF16).ap()
    x1_s = nc.dram_tensor("x1_s", [T, D], F32).ap()

    P = Prog(nc)
    cnt = [0]

    def sbt(name, shape, dt, off):
        n = 1
        for s in shape[1:]:
            n *= s
        nb = n * (4 if dt == F32 else 2)
        cnt[0] += 1
        t = nc.alloc_sbuf_tensor_at("%s_%d" % (name, cnt[0]), shape, dt, offset=off + 16512)
        return t, off + ((nb + 31) // 32) * 32

    class Alloc:
        def __init__(self, off):
            self.off = off

        def __call__(self, name, shape, dt=F32):
            t, self.off = sbt(name, shape, dt, self.off)
            assert self.off <= 212800, (name, self.off)
            return t

    def mm(o, l, r, st, sp, R, W):
        P.add(PE, lambda h: h.matmul(o, l, r, start=st, stop=sp), R, W)

    def tr(o, i, R, W):
        P.add(PE, lambda h: h.transpose(o, i, ident[:]), list(R) + ["ident"], W)

    def act(o, i, f, R, W, bias=None, scale=None, accum=None):
        kw = {}
        if bias is not None:
            kw["bias"] = bias
        if scale is not None:
            kw["scale"] = scale
        if accum is not None:
            kw["accum_out"] = accum
        P.add(ACT, lambda h: h.activation(out=o, in_=i, func=f, **kw), R, W)

    def tt(e, o, a, b, op, R, W):
        P.add(e, lambda h: h.tensor_tensor(out=o, in0=a, in1=b, op=op), R, W)

    def ts(e, o, a, s1, s2, op0, op1, R, W):
        if s2 is None:
            P.add(e, lambda h: h.tensor_scalar(out=o, in0=a, scalar1=s1, scalar2=None, op0=op0), R, W)
        else:
            P.add(e, lambda h: h.tensor_scalar(out=o, in0=a, scalar1=s1, scalar2=s2, op0=op0, op1=op1), R, W)

    def stt(e, o, a, s, b, op0, op1, R, W, accum=None):
        if accum is None:
            P.add(e, lambda h: h.scalar_tensor_tensor(out=o, in0=a, scalar=s, in1=b, op0=op0, op1=op1), R, W)
        else:
            P.add(e, lambda h: h.scalar_tensor_tensor(out=o, in0=a, scalar=s, in1=b, op0=op0, op1=op1, accum_out=accum), R, W)

    def cp(e, o, i, R, W):
        if e == ACT:
            P.add(ACT, lambda h: h.copy(out=o, in_=i), R, W)
        else:
            P.add(e, lambda h: h.tensor_copy(out=o, in_=i), R, W)

    def ms(e, o, v, W):
        P.add(e, lambda h: h.memset(o, v), [], W)

    def dma(q, o, i, R, W, slow=False):
        if slow:
            P.add(q, lambda h: h.dma_start(out=o, in_=i, allow_slow_non_contiguous=True), R, W, dma=True)
        else:
            P.add(q, lambda h: h.dma_start(out=o, in_=i), R, W, dma=True)

    def asel(o, pat, cm, op, W):
        P.add(POOL, lambda h: h.affine_select(out=o, in_=o, pattern=pat, compare_op=op, fill=0.0,
                                              base=0, channel_multiplier=cm), W, W)

    pd = [nc.alloc_psum_tensor("pd%d" % i, [128, 1024], F32) for i in range(4)]

    def bank(i):
        return pd[i // 2][:, (i % 2) * 512:(i % 2 + 1) * 512]

    def bankb(i):
        return bank(i).bitcast(BF16)

    def pk(i):
        return "ps%d" % i

    CA = Alloc(0)
    ident = CA("ident", [128, 128], BF16); identf = CA("identf", [128, 128])
    Umat = CA("Umat", [128, 128]); Lmat = CA("Lmat", [128, 128]); ones = CA("ones", [128, 128])
    SLf = CA("SLf", [128, 128], BF16); SLb = CA("SLb", [128, 128], BF16)
    Ubf = CA("Ubf", [128, 128], BF16); Lbf = CA("Lbf", [128, 128], BF16)
    mLs = CA("mLs", [128, 128]); mtmp = CA("mtmp", [128, 128])
    WsT = CA("WsT", [128, 8, 128], BF16)
    cw_t = CA("cw", [128, 12, 5]); cb_t = CA("cb", [128, 12])
    n1w_t = CA("n1w", [128, 8]); n2w_t = CA("n2w", [128, 8]); snw_t = CA("snw", [128, 8])
    expA = CA("expA", [128, 32]); dtb_t = CA("dtb", [128, 32]); dD_t = CA("dD", [128, 16])
    eps1 = CA("eps1", [128, 1]); eps4 = CA("eps4", [128, 1])
    sc_t = CA("sc", [128, 8, NJ])
    modT = [CA("modT%d" % j, [128, 48]) for j in range(NJ)]
    scale1 = [CA("scale1_%d" % j, [128, 8]) for j in range(NJ)]
    scale2 = [CA("scale2_%d" % j, [128, 8]) for j in range(NJ)]
    CEND = CA.off
    assert CEND <= 9216, CEND
    RA, RB, ST, WB = 9216, 58496, 91264, 105600

    def mask(dst_f32, pat, cm, op, key):
        ms(POOL, dst_f32[:], 1.0, [key])
        asel(dst_f32[:], pat, cm, op, [key])
    mask(identf, [[1, 128]], -1, ALU.is_equal, "identf")
    mask(Umat, [[1, 128]], -1, ALU.is_ge, "Umat")
    mask(Lmat, [[-1, 128]], 1, ALU.is_ge, "Lmat")
    mask(mLs, [[-1, 128]], 1, ALU.is_gt, "mLs")
    mask(mtmp, [[1, 128]], -1, ALU.is_gt, "mtmp")
    ms(POOL, ones[:], 1.0, ["ones"])
    cp(DVE, ident[:], identf[:], ["identf"], ["ident"])
    cp(DVE, Ubf[:], Umat[:], ["Umat"], ["Ubf"])
    cp(DVE, Lbf[:], Lmat[:], ["Lmat"], ["Lbf"])
    cp(DVE, SLf[:], mLs[:], ["mLs"], ["SLf"])
    cp(DVE, SLb[:], mtmp[:], ["mtmp"], ["SLb"])
    ms(POOL, eps1[:], EPS, ["eps1"]); ms(POOL, eps4[:], 4 * EPS, ["eps4"])
    dma(POOL, WsT[:], wsT, [], ["WsT"])
    dma(SP, cw_t[:], cw, [], ["cw"]); dma(SP, cb_t[:], cbv, [], ["cb"])
    dma(SP, n1w_t[:], n1w, [], ["n1w"]); dma(SP, n2w_t[:], n2w, [], ["n2w"]); dma(SP, snw_t[:], snw, [], ["snw"])
    dma(SP, expA[:], prow(alog), [], ["expA"]); dma(SP, dtb_t[:], prow(dtbias), [], ["dtb"])
    dma(SP, dD_t[:], prow(dD), [], ["dD"])
    act(expA[:], expA[:], AF.Exp, ["expA"], ["expA"])
    dma(SP, sc_t[:], cT, [], ["sc"])
    act(sc_t[:], sc_t[:], AF.Silu, ["sc"], ["sc"])

    wA, _ = sbt("wA", [128, 8, NA], BF16, WB)
    dma(POOL, wA[:], w_in.rearrange("(k p) n -> p k n", p=128)[:, :, X0:X0 + NA], [], ["wA"])
    A0 = Alloc(RA)
    wblk = [A0("wblk%d" % i, [128, 3072]) for i in range(2)]
    bblk = A0("bblk", [NJ, 3072])
    mblk = A0("mblk", [NJ, 3072])
    it = 0
    for half in range(2):
        c0 = half * 3072
        dma(SP, bblk[:], prow(b_ada[:, c0:c0 + 3072], NJ), [], ["bblk"])
        for k in range(8):
            i = it % 2; it += 1
            dma(SP, wblk[i][:], w_ada[k * 128:(k + 1) * 128, c0:c0 + 3072], [], [("wblk", i)])
            for cb in range(6):
                mm(bank(cb)[0:NJ, :], sc_t[:, k, :], wblk[i][:, cb * 512:(cb + 1) * 512], k == 0, k == 7, ["sc", ("wblk", i)], [pk(cb)])
        for cb in range(6):
            tt(DVE, mblk[:, cb * 512:(cb + 1) * 512], bank(cb)[0:NJ, :], bblk[:, cb * 512:(cb + 1) * 512], ALU.add,
               [pk(cb), "bblk"], ["mblk"])
        dma(SP, m_s[:, c0:c0 + 3072], mblk[:], ["mblk"], ["m_s"])
    for j in range(NJ):
        dma(SP, modT[j][:], m_s[j:j + 1, :].rearrange("o (c p) -> p (o c)", p=128), ["m_s"], [("modT", j)], slow=True)
        stt(DVE, scale1[j][:], modT[j][:, 8:16], 1.0, n1w_t[:], ALU.add, ALU.mult, [("modT", j), "n1w"], [("scale1", j)])
        stt(DVE, scale2[j][:], modT[j][:, 32:40], 1.0, n2w_t[:], ALU.add, ALU.mult, [("modT", j), "n2w"], [("scale2", j)])
    P.barrier()

    def norm_to_T(src, L, scale_ap, shift_ap, mkeys, dst, dkey, xseg, xn, junk, sss, kb=0):
        xsegs = xseg if isinstance(xseg, list) else [xseg]
        xns = xn if isinstance(xn, list) else [xn]
        for s0 in range(0, L, 512):
            ln = min(512, L - s0)
            nt = ln // 128
            sp_ = (s0 // 512) % len(xsegs)
            kq = sp_ + kb
            xseg = xsegs[sp_]
            xn = xns[sp_ % len(xns)]
            xp_ = sp_ % len(xns) + kb
            ss = sss[sp_] if isinstance(sss, list) else sss
            ms(DVE, ss[:, 0:8], 0.0, [("ss", kq)])
            for i in range(nt):
                dma(SP, xseg[:, i, :], src[s0 + i * 128:s0 + (i + 1) * 128, :], [], [("xseg", kq, i)])
                act(junk[:] if junk is not None else xn[:, i, :], xseg[:, i, :], AF.Square, [("xseg", kq, i)],
                    ["junk" if junk is not None else ("xn", xp_, i), ("ss", kq)], accum=ss[:, i:i + 1])
            act(ss[:, 4:8], ss[:, 0:4], AF.Ln, [("ss", kq), "eps1"], [("ss", kq)], bias=eps1[:], scale=1.0 / D)
            act(ss[:, 8:12], ss[:, 4:8], AF.Exp, [("ss", kq)], [("rstd", kq)], scale=-0.5)
            for i in range(nt):
                if i % 2 == 0:
                    act(xn[:, i, :], xseg[:, i, :], AF.Identity, [("xseg", kq, i), ("rstd", kq)], [("xn", xp_, i)], scale=ss[:, 8 + i:9 + i])
                else:
                    ts(DVE, xn[:, i, :], xseg[:, i, :], ss[:, 8 + i:9 + i], None, ALU.mult, None,
                       [("xseg", kq, i), ("rstd", kq)], [("xn", xp_, i)])
            for k in range(8):
                pv = bankb(k // 2)[:, (k % 2) * 512:(k % 2) * 512 + ln]
                for i in range(nt):
                    tr(pv[:, i * 128:(i + 1) * 128], xn[:, i, k * 128:(k + 1) * 128], [("xn", xp_, i)], [pk(k // 2)])
                if k % 2 == 0:
                    act(dst[:, k, s0:s0 + ln], pv, AF.Identity, [pk(k // 2)] + mkeys, [dkey],
                        bias=shift_ap[:, k:k + 1], scale=scale_ap[:, k:k + 1])
                else:
                    ts(DVE, dst[:, k, s0:s0 + ln], pv, scale_ap[:, k:k + 1], shift_ap[:, k:k + 1], ALU.mult, ALU.add,
                       [pk(k // 2)] + mkeys, [dkey])

    def sweepA(hlT, L, wA, xbcT, dtraw, only=None):
        rr = 0
        for s0 in ([only] if only is not None else range(0, L, 512)):
            ln = min(512, L - s0)
            for j in range(12):
                b = 4 + (rr % 4); rr += 1
                for k in range(8):
                    mm(bank(b)[:, 0:ln], wA[:, k, j * 128:(j + 1) * 128], hlT[:, k, s0:s0 + ln], k == 0, k == 7,
                       ["wA", "hlT"], [pk(b)])
                cp(ACT if j % 2 == 0 else DVE, xbcT[:, j, 2 + s0:2 + s0 + ln], bank(b)[:, 0:ln], [pk(b)], [("xbc", j)])
            for i in range(ln // 128):
                b = 4 + (rr % 4); rr += 1
                t0 = s0 + i * 128
                for k in range(8):
                    mm(bank(b)[:, 0:32], hlT[:, k, t0:t0 + 128], wA[:, k, 1536:1568], k == 0, k == 7, ["wA", "hlT"], [pk(b)])
                cp(DVE, dtraw[:, t0 // 128, :], bank(b)[:, 0:32], [pk(b)], ["dtraw"])

    def conv_stage(L, xbcT, dg, xsh, xs_tok, B_tok):
        nt = L // 128
        for j in range(12):
            for k in range(5):
                ts(DVE, dg[:, (j * 5 + k) * 128:(j * 5 + k + 1) * 128], ident[:], cw_t[:, j, k:k + 1], None, ALU.mult, None,
                   ["ident", "cw"], [("dg", j)])
            cp(POOL if j % 2 else DVE, xsh[:, j, 0:L + 2], xbcT[:, j, 1:L + 3], [("xbc", j)], [("xsh", j)])
        for j in range(12):
            segs = [(s0, min(512, L - s0)) for s0 in range(0, L, 512)]
            for si, (s0, ln) in enumerate(segs):
                bk = (j % 2) * 4 + si
                for k in range(5):
                    src = xbcT[:, j, s0 + k:s0 + k + ln] if k % 2 == 0 else xsh[:, j, s0 + k - 1:s0 + k - 1 + ln]
                    mm(bank(bk)[:, 0:ln], dg[:, (j * 5 + k) * 128:(j * 5 + k + 1) * 128], src,
                       k == 0, k == 4, [("dg", j), ("xbc", j), ("xsh", j)], [pk(bk)])
            for si, (s0, ln) in enumerate(segs):
                bk = (j % 2) * 4 + si
                act(xbcT[:, j, 2 + s0:2 + s0 + ln], bank(bk)[:, 0:ln], AF.Silu, [pk(bk), "cb"], [("xbc", j)], bias=cb_t[:, j:j + 1])
        for i in range(nt):
            pv = bankb(i % 2)
            for j in range(8):
                tr(pv[:, j * 128:(j + 1) * 128], xbcT[:, j, 2 + i * 128:2 + (i + 1) * 128], [("xbc", j)], [pk(i % 2)])
            cp(ACT if i % 2 == 0 else DVE, xs_tok[:, i, :], pv, [pk(i % 2)], [("xs", i)])
        for i0 in range(0, nt, 4):
            n = min(4, nt - i0)
            pv = bankb(2 + (i0 // 4) % 2)
            for ii in range(n):
                for g in range(2):
                    tr(pv[:, ii * 256 + g * 128:ii * 256 + (g + 1) * 128],
                       xbcT[:, 8 + g, 2 + (i0 + ii) * 128:2 + (i0 + ii + 1) * 128], [("xbc", 8 + g)], [pk(2 + (i0 // 4) % 2)])
            cp(DVE, B_tok[:, i0:i0 + n, :], pv[:, 0:n * 256].rearrange("p (a b) -> p a b", b=256),
               [pk(2 + (i0 // 4) % 2)], ["Btok"])

    def dt_stage(L, dtraw, dta):
        nt = L // 128
        dt, lndt, aa, csa, tot, wts = dta["dt"], dta["lndt"], dta["a"], dta["cs"], dta["tot"], dta["wts"]
        tt(DVE, dt[:], dtraw[:], bcm(dtb_t[:], nt), ALU.add, ["dtraw", "dtb"], ["dt"])
        act(dt[:], dt[:], AF.Exp, ["dt"], ["dt"])
        act(dt[:], dt[:], AF.Ln, ["dt", "ones"], ["dt"], bias=ones[:, 0:1])
        act(lndt[:], dt[:], AF.Ln, ["dt"], ["lndt"])
        stt(DVE, aa[:], dt[:], -1.0, bcm(expA[:], nt), ALU.mult, ALU.mult, ["dt", "expA"], ["a"])
        for i in range(nt):
            mm(bank(6)[:, i * 32:i * 32 + 16], Umat[:], aa[:, i, 0:16], True, True, ["Umat", "a"], [pk(6)])
            mm(bank(6)[:, i * 32 + 16:i * 32 + 32], Lmat[:], aa[:, i, 16:32], True, True, ["Lmat", "a"], [pk(6)])
            mm(bank(7)[:, i * 32:i * 32 + 32], ones[:], aa[:, i, :], True, True, ["ones", "a"], [pk(7)])
        c3 = bank(6)[:, 0:nt * 32].rearrange("p (a b) -> p a b", b=32)
        t3 = bank(7)[:, 0:nt * 32].rearrange("p (a b) -> p a b", b=32)
        cp(DVE, csa[:], c3, [pk(6)], ["cs"])
        cp(DVE, tot[:], t3, [pk(7)], ["tot"])
        tt(DVE, wts[:], tot[:], csa[:], ALU.subtract, ["tot", "cs"], ["wts"])
        tt(DVE, wts[:], wts[:], lndt[:], ALU.add, ["wts", "lndt"], ["wts"])
        act(wts[:], wts[:], AF.Exp, ["wts"], ["wts"])
        act(csa[:], csa[:], AF.Exp, ["cs"], ["cs"])
        act(tot[:], tot[:], AF.Exp, ["tot"], ["tot"])

    def state_pass(nt, order, col0, xs_tok, B_tok, xbcT, dta, S, Sbf, xw, tmpf, ybuf=None, ykey=None):
        order = list(order)

        def prep(idx):
            c = order[idx]
            tt(DVE, xw[idx % 2][:].rearrange("p (h e) -> p h e", e=64), xs_tok[:, c, :].rearrange("p (h e) -> p h e", e=64),
               bcl(dta["wts"][:, c, col0:col0 + 16], 64), ALU.mult, [("xs", c), "wts"], [("xw", idx % 2)])

        prep(0)
        for idx, c in enumerate(order):
            tsl = slice(2 + c * 128, 2 + (c + 1) * 128)
            x2 = xw[idx % 2]
            if ybuf is not None:
                for g in range(2):
                    mm(bank(g), xbcT[:, 10 + g, tsl], Sbf[:, g * 512:(g + 1) * 512], True, True,
                       [("xbc", 10 + g), "Sbf"], [pk(g)])
            for g in range(2):
                mm(bank(2 + g), B_tok[:, c, g * 128:(g + 1) * 128], x2[:, g * 512:(g + 1) * 512], True, True,
                   ["Btok", ("xw", idx % 2)], [pk(2 + g)])
            tt(DVE, tmpf[:].rearrange("p (h e) -> p h e", e=64), S[:].rearrange("p (h e) -> p h e", e=64),
               bcl(dta["tot"][:, c, col0:col0 + 16], 64), ALU.mult, ["S", "tot"], ["tmpS"])
            for g in range(2):
                tt(DVE, S[:, g * 512:(g + 1) * 512], tmpf[:, g * 512:(g + 1) * 512], bank(2 + g), ALU.add,
                   ["tmpS", pk(2 + g)], ["S"])
            cp(ACT, Sbf[:], S[:], ["S"], ["Sbf"])
            if ybuf is not None:
                for g in range(2):
                    tt(DVE, ybuf[:, c, g * 512:(g + 1) * 512].rearrange("p (h e) -> p h e", e=64),
                       bank(g).rearrange("p (h e) -> p h e", e=64),
                       bcl(dta["cs"][:, c, col0 + g * 8:col0 + g * 8 + 8], 64), ALU.mult, [pk(g), "cs"], [(ykey, c)])
            if idx + 1 < len(order):
                prep(idx + 1)

    xbcT_t, _ = sbt("xbcT", [128, 12, T + 4], BF16, RA)
    RBt, _ = sbt("RB", [128, 8 * T], BF16, RB)
    hlT = RBt[:].rearrange("p (k t) -> p k t", k=8)
    xs_tok = RBt[:].rearrange("p (i f) -> p i f", f=D)
    SA = Alloc(ST)
    Hs = SA("Hs", [128, D]); Gs = SA("Gs", [128, D])
    Hbf = SA("Hbf", [128, D], BF16); Gbf = SA("Gbf", [128, D], BF16)
    dtraw = SA("dtraw", [128, NT, 32])
    assert SA.off <= WB

    for b in range(NB):
        A1 = Alloc(WB + 8 * NA * 2)
        if b > 0:
            dma(POOL, wA[:], w_in.rearrange("(k p) n -> p k n", p=128)[:, :, X0:X0 + NA], [], ["wA"])
        markW = A1.off
        hlT_c = A1("hlT_c", [128, 8, TC], BF16)
        xbcT_c = A1("xbcT_c", [128, 12, TC + 4], BF16)
        xs_c = A1("xs_c", [128, NTC, D], BF16)
        B_c = A1("B_c", [128, NTC, 256], BF16)
        dtraw_c = A1("dtraw_c", [128, NTC, 32])
        dta_c = {n: A1("c_" + n, [128, NTC, 32]) for n in ["dt", "lndt", "a", "cs", "tot", "wts"]}
        xseg = A1("xseg", [128, 4, D]); xn = A1("xn", [128, 4, D], BF16)
        junk = A1("junk", [128, D], BF16); ss = A1("ss", [128, 16])
        dg = A1("dg", [128, 60 * 128], BF16)
        xsh_c = A1("xsh_c", [128, 12, TC + 4], BF16)
        xw = [A1("xw%d" % i, [128, D], BF16) for i in range(2)]
        tmpf = A1("tmpf", [128, D])
        ms(POOL, xbcT_c[:, :, 0:2], 0.0, [("xbc", j) for j in range(12)])
        ms(POOL, xbcT_c[:, :, TC + 2:TC + 4], 0.0, [("xbc", j) for j in range(12)])
        norm_to_T(ctx[b], TC, scale1[NB], modT[NB], [("scale1", NB), ("modT", NB)], hlT_c, "hlT", xseg, xn, junk, ss)
        sweepA(hlT_c, TC, wA, xbcT_c, dtraw_c)
        dt_stage(TC, dtraw_c, dta_c)
        conv_stage(TC, xbcT_c, dg, xsh_c, xs_c, B_c)
        ms(DVE, Hs[:], 0.0, ["S"]); ms(DVE, Gs[:], 0.0, ["S"])
        state_pass(NTC, range(NTC), 0, xs_c, B_c, xbcT_c, dta_c, Hs, Hbf, xw, tmpf)
        P.barrier()
        state_pass(NTC, range(NTC - 1, -1, -1), 16, xs_c, B_c, xbcT_c, dta_c, Gs, Gbf, xw, tmpf)
        P.barrier()

        A2 = Alloc(markW)
        wb1 = A2("wb1", [128, 8, 2048], BF16)
        mark = A2.off
        xseg = [A2("xseg%d" % i, [128, 4, D]) for i in range(2)]
        xn = A2("xn", [128, 4, D], BF16)
        junk = A2("junk", [128, D], BF16); ss = [A2("ss%d" % i, [128, 16]) for i in range(2)]
        dma(POOL, wb1[:], w_in.rearrange("(k p) n -> p k n", p=128)[:, :, 0:2048], [], ["wb1"])
        norm_to_T(x[b], T, scale1[b], modT[b], [("scale1", b), ("modT", b)], hlT, "hlT", xseg, xn, junk, ss)
        P.barrier()

        A3 = Alloc(mark)
        gnw_bc = A3("gnw_bc", [128, D]); bs_bc = A3("bs_bc", [128, D])
        dma(SP, gnw_bc[:], prow(gnw), [], ["gnw_bc"]); dma(SP, bs_bc[:], prow(bsv), [], ["bs_bc"])
        uT = A3("uT", [128, 8, 512], BF16); vv = A3("vv", [128, 4, D], BF16)
        sq = A3("sq", [128, D], BF16); vnf = A3("vnf", [128, D]); vn2 = A3("vn2", [128, D], BF16)
        mxt = A3("mxt", [128, D]); gmT = [A3("gmT%d" % i, [128, D], BF16) for i in range(2)]
        zt = [A3("zt%d" % i, [128, D], BF16) for i in range(2)]
        tz = A3("tz", [128, 512]); ssv = A3("ssv", [128, 96])
        ms(POOL, xbcT_t[:, :, 0:2], 0.0, [("xbc", j) for j in range(12)])
        ms(POOL, xbcT_t[:, :, T + 2:T + 4], 0.0, [("xbc", j) for j in range(12)])
        rr = 0
        for s0 in range(0, T, 512):
            sweepA(hlT, T, wA, xbcT_t, dtraw, only=s0)
            for h in range(8):
                bk = 4 + rr % 4; rr += 1
                for k in range(8):
                    mm(bank(bk), wb1[:, k, h * 128:(h + 1) * 128], hlT[:, k, s0:s0 + 512], k == 0, k == 7, ["wb1", "hlT"], [pk(bk)])
                act(uT[:, h, :], bank(bk), AF.Gelu, [pk(bk)], ["uT"])
            for i in range(4):
                t0 = s0 + i * 128
                for hf in range(2):
                    bk = 4 + rr % 4; rr += 1
                    for k in range(8):
                        mm(bank(bk), hlT[:, k, t0:t0 + 128], wb1[:, k, 1024 + hf * 512:1024 + (hf + 1) * 512], k == 0, k == 7,
                           ["wb1", "hlT"], [pk(bk)])
                    act(vv[:, i, hf * 512:(hf + 1) * 512], bank(bk), AF.Gelu, [pk(bk)], [("vv", i)])
                tt(DVE, sq[:], vv[:, i, :], vv[:, i, :], ALU.mult, [("vv", i)], ["sq"])
                P.add(DVE, (lambda o, a: (lambda h_: h_.tensor_reduce(out=o, in_=a, axis=AX.X, op=ALU.add)))(
                    ssv[:, i * 8:(i + 1) * 8], sq[:].rearrange("p (h e) -> p h e", e=128)), ["sq"], ["ssv"])
            act(ssv[:, 32:64], ssv[:, 0:32], AF.Ln, ["ssv", "eps1"], ["ssv"], bias=eps1[:], scale=1.0 / 128)
            act(ssv[:, 64:96], ssv[:, 32:64], AF.Exp, ["ssv"], ["rsv"], scale=-0.5)
            for i in range(4):
                t0 = s0 + i * 128
                tt(DVE, vnf[:].rearrange("p (h e) -> p h e", e=128), vv[:, i, :].rearrange("p (h e) -> p h e", e=128),
                   bcl(ssv[:, 64 + i * 8:64 + (i + 1) * 8], 128), ALU.mult, [("vv", i), "rsv"], ["vnf"])
                tt(DVE, vn2[:], vnf[:], gnw_bc[:], ALU.mult, ["vnf", "gnw_bc"], ["vn2"])
                for h in range(8):
                    mm(bank(h // 4)[:, (h % 4) * 128:(h % 4 + 1) * 128], vn2[:, h * 128:(h + 1) * 128], WsT[:, h, :], True, True,
                       ["vn2", "WsT"], [pk(h // 4)])
                g_ = gmT[i % 2]
                for hf in range(2):
                    tt(DVE, mxt[:, hf * 512:(hf + 1) * 512], bank(hf), bs_bc[:, hf * 512:(hf + 1) * 512], ALU.add,
                       [pk(hf), "bs_bc"], ["mxt"])
                tt(DVE, g_[:].rearrange("p (h e) -> p h e", e=128), mxt[:].rearrange("p (h e) -> p h e", e=128),
                   uT[:, :, i * 128:(i + 1) * 128], ALU.mult, ["mxt", "uT"], [("gmT", i % 2)])
                dma(POOL, gm_s[t0 // 128], g_[:], [("gmT", i % 2)], [("gm_s", t0 // 128)])
        dma(POOL, wb1[:, :, 0:1024], w_in.rearrange("(k p) n -> p k n", p=128)[:, :, Z0:Z0 + 1024], [], ["wb1"])
        for i in range(NT):
            t0 = i * 128
            z_ = zt[i % 2]
            for hf in range(2):
                bk = 4 + rr % 4; rr += 1
                for k in range(8):
                    mm(bank(bk), hlT[:, k, t0:t0 + 128], wb1[:, k, hf * 512:(hf + 1) * 512], k == 0, k == 7, ["wb1", "hlT"], [pk(bk)])
                act(tz[:], bank(bk), AF.Tanh, [pk(bk)], ["tz"], scale=0.5)
                stt(DVE, z_[:, hf * 512:(hf + 1) * 512], tz[:], 1.0, bank(bk), ALU.add, ALU.mult, ["tz", pk(bk)], [("zt", i % 2)])
            dma(POOL, z_s[t0:t0 + 128, :], z_[:], [("zt", i % 2)], [("z_s", i)])
        P.barrier()

        A4 = Alloc(WB)
        B_tok = A4("B_tok", [128, NT, 256], BF16)
        dta = {n: A4("l_" + n, [128, NT, 32]) for n in ["dt", "lndt", "a", "cs", "tot", "wts"]}
        dl = A4("dl", [128, NT, 16]); coef = A4("coef", [128, NT, 16]); cbd = A4("cbd", [128, NT, 2])
        mark5 = A4.off
        dg = A4("dg", [128, 60 * 128], BF16)
        xsh = A4("xsh", [128, 12, T + 4], BF16)
        dt_stage(T, dtraw, dta)
        conv_stage(T, xbcT_t, dg, xsh, xs_tok, B_tok)
        tt(DVE, dl[:], dta["lndt"][:, :, 16:32], dta["lndt"][:, :, 0:16], ALU.subtract, ["lndt"], ["dl"])
        P.barrier()

        A5 = Alloc(mark5)
        xw = [A5("xw%d" % i, [128, D], BF16) for i in range(2)]
        tmpf = A5("tmpf", [128, D])
        ybt, _ = sbt("ybuf", [128, NT, D], BF16, RA)
        state_pass(NT, range(NT - 1, -1, -1), 16, xs_tok, B_tok, xbcT_t, dta, Gs, Gbf, xw, tmpf, ybuf=ybt, ykey="yb")
        P.barrier()

        rf_ = A5("rf", [128, 16 * 128], BF16); rb_ = A5("rb", [128, 16 * 128], BF16)
        rf = [rf_, rf_]; rb = [rb_, rb_]
        wo, _ = sbt("wo", [128, 16, D], BF16, 180032)
        dma(POOL, wo[:], w_out.rearrange("(k p) n -> p k n", p=128), [], ["wo"])
        Mf = [A5("Mf%d" % i, [128, 16 * 128], BF16) for i in range(2)]
        Mb = [A5("Mb%d" % i, [128, 16 * 128], BF16) for i in range(2)]
        E4_ = A5("E4", [128, 512], BF16); E4 = [E4_, E4_]
        t1b = [A5("t1b%d" % i, [128, D], BF16) for i in range(2)]
        xdf = [A5("xdf%d" % i, [128, D], BF16) for i in range(2)]
        xdb = [A5("xdb%d" % i, [128, D], BF16) for i in range(2)]
        xc = xw[1]
        cbf = A5("cbf", [128, 256], BF16); cbb = A5("cbb", [128, 256], BF16); junkf = A5("junkf", [128, 128])
        t1 = A5("t1", [128, D], BF16); sz_ = A5("sz", [128, D], BF16); szt = [sz_, sz_]
        gn = xw[1]; ssg = A5("ssg", [128, 8]); junk = xc
        assert A5.off <= 180032, A5.off

        def v64(a):
            return a.rearrange("p (h e) -> p h e", e=64)

        def A_yoff(c):
            tsl = slice(2 + c * 128, 2 + (c + 1) * 128)
            for g in range(2):
                mm(bank(g), xbcT_t[:, 10 + g, tsl], Hbf[:, g * 512:(g + 1) * 512], True, True, [("xbc", 10 + g), "Sbf"], [pk(g)])

        def A_dve_nodep(c):
            p = c % 2
            tt(DVE, v64(xw[0][:]), v64(xs_tok[:, c, :]), bcl(dta["wts"][:, c, 0:16], 64), ALU.mult, [("xs", c), "wts"], [("xw", 0)])
            tt(DVE, v64(tmpf[:]), v64(Hs[:]), bcl(dta["tot"][:, c, 0:16], 64), ALU.mult, ["S", "tot"], ["tmpS"])
            tt(DVE, v64(xdf[p][:]), v64(xs_tok[:, c, :]), bcl(dta["dt"][:, c, 0:16], 64), ALU.mult, [("xs", c), "dt"], [("xdf", p)])
            tt(DVE, v64(xdb[p][:]), v64(xs_tok[:, c, :]), bcl(dta["dt"][:, c, 16:32], 64), ALU.mult, [("xs", c), "dt"], [("xdb", p)])

        def A_pe2(c):
            tsl = slice(2 + c * 128, 2 + (c + 1) * 128)
            for g in range(2):
                mm(bank(3 + g), B_tok[:, c, g * 128:(g + 1) * 128], xw[0][:, g * 512:(g + 1) * 512], True, True,
                   ["Btok", ("xw", 0)], [pk(3 + g)])
            for g in range(2):
                mm(bank(2)[:, g * 128:(g + 1) * 128], xbcT_t[:, 8 + g, tsl], xbcT_t[:, 10 + g, tsl], True, True,
                   [("xbc", 8 + g), ("xbc", 10 + g)], [pk(2)])

        def A_dve_dep(c):
            p = c % 2
            for g in range(2):
                tt(DVE, v64(t1b[p][:, g * 512:(g + 1) * 512]), v64(bank(g)),
                   bcl(dta["cs"][:, c, g * 8:g * 8 + 8], 64), ALU.mult, [pk(g), "cs"], [("t1b", p)])
            for g in range(2):
                tt(DVE, Hs[:, g * 512:(g + 1) * 512], tmpf[:, g * 512:(g + 1) * 512], bank(3 + g), ALU.add, ["tmpS", pk(3 + g)], ["S"])
            cp(ACT, Hbf[:], Hs[:], ["S"], ["Sbf"])
            c3 = bank(2)[:, 0:256].rearrange("p (g t) -> p g t", t=128)
            tt(DVE, cbf[:].rearrange("p (g t) -> p g t", t=128), c3, bcm(Umat[:], 2), ALU.mult, [pk(2), "Umat"], ["cbf"])
            tt(DVE, cbb[:].rearrange("p (g t) -> p g t", t=128), c3, bcm(mLs[:], 2), ALU.mult, [pk(2), "mLs"], ["cbb"])
            for g in range(2):
                tt(DVE, junkf[:], bank(2)[:, g * 128:(g + 1) * 128], identf[:], ALU.mult, [pk(2), "identf"], ["junkf"])
                P.add(DVE, (lambda o, a_: (lambda h_: h_.tensor_reduce(out=o, in_=a_, axis=AX.X, op=ALU.add)))(
                    cbd[:, c, g:g + 1], junkf[:]), ["junkf"], [("cbd", c)])

        def A_builds(c):
            p = c % 2
            for h in range(16):
                act(rf[p][:, h * 128:(h + 1) * 128], Ubf[:], AF.Identity, ["Ubf", "a"], [("rf", 0, h // 4)], scale=dta["a"][:, c, h:h + 1])
                act(rb[p][:, h * 128:(h + 1) * 128], Lbf[:], AF.Identity, ["Lbf", "a"], [("rb", 0, h // 4)], scale=dta["a"][:, c, 16 + h:17 + h])

        def A_decay(c):
            p = c % 2
            for q in range(4):
                bk = 3 + q % 2
                mm(bank(bk), SLf[:], rf[p][:, q * 512:(q + 1) * 512], True, False, ["SLf", ("rf", 0, q)], [pk(bk)])
                mm(bank(bk), SLb[:], rb[p][:, q * 512:(q + 1) * 512], False, True, ["SLb", ("rb", 0, q)], [pk(bk)])
                act(E4[q % 2][:], bank(bk), AF.Exp, [pk(bk)], [("E4", 0)])
                g = q // 2
                tt(DVE, Mf[p][:, q * 512:(q + 1) * 512].rearrange("p (h t) -> p h t", t=128),
                   E4[q % 2][:].rearrange("p (h t) -> p h t", t=128), bcm(cbf[:, g * 128:(g + 1) * 128], 4), ALU.mult,
                   [("E4", 0), "cbf"], [("Mf", p, q)])
                tt(DVE, Mb[p][:, q * 512:(q + 1) * 512].rearrange("p (h t) -> p h t", t=128),
                   E4[q % 2][:].rearrange("p (h t) -> p h t", t=128), bcm(cbb[:, g * 128:(g + 1) * 128], 4), ALU.mult,
                   [("E4", 0), "cbb"], [("Mb", p, q)])

        def B_front(c):
            p = c % 2
            for g in range(2):
                stt(DVE, coef[:, c, g * 8:(g + 1) * 8], dta["dt"][:, c, 16 + g * 8:16 + (g + 1) * 8], cbd[:, c, g:g + 1],
                    dD_t[:, g * 8:(g + 1) * 8], ALU.mult, ALU.add, ["dt", ("cbd", c), "dD"], [("coef", c)])
            tt(DVE, v64(xc[:]), v64(xs_tok[:, c, :]), bcl(coef[:, c, :], 64), ALU.mult, [("xs", c), ("coef", c)], [("xw", 1)])
            for hb in range(2):
                bk = 5 + hb
                cs_ = slice(hb * 512, (hb + 1) * 512)
                mm(bank(bk), ident[:], t1b[p][:, cs_], True, False, ["ident", ("t1b", p)], [pk(bk)])
                mm(bank(bk), ident[:], ybt[:, c, cs_], False, False, ["ident", ("yb", c)], [pk(bk)])
                mm(bank(bk), ident[:], xc[:, cs_], False, False, ["ident", ("xw", 1)], [pk(bk)])
                for hh in range(8):
                    H = hb * 8 + hh
                    o = bank(bk)[:, hh * 64:(hh + 1) * 64]
                    mm(o, Mf[p][:, H * 128:(H + 1) * 128], xdf[p][:, H * 64:(H + 1) * 64], False, False,
                       [("Mf", p, H // 4), ("xdf", p)], [pk(bk)])
                    mm(o, Mb[p][:, H * 128:(H + 1) * 128], xdb[p][:, H * 64:(H + 1) * 64], False, hh == 7,
                       [("Mb", p, H // 4), ("xdb", p)], [pk(bk)])

        def B_gate(c):
            for hb in range(2):
                cs_ = slice(hb * 512, (hb + 1) * 512)
                tt(DVE, t1[:, cs_], bank(5 + hb), szt[0][:, cs_], ALU.mult, [pk(5 + hb), "sz"], ["t1g"])

        def B_norm(c):
            ms(DVE, ssg[:, 0:2], 0.0, ["ssg"])
            for g in range(2):
                act(junk[:, 0:512], t1[:, g * 512:(g + 1) * 512], AF.Square, ["t1g"], [("xw", 1), "ssg"], accum=ssg[:, g:g + 1])
            act(ssg[:, 2:4], ssg[:, 0:2], AF.Ln, ["ssg", "eps4"], ["ssg"], bias=eps4[:], scale=1.0 / 512)
            act(ssg[:, 4:6], ssg[:, 2:4], AF.Exp, ["ssg"], ["rsg"], scale=-0.5)
            for g in range(2):
                act(gn[:, g * 512:(g + 1) * 512], t1[:, g * 512:(g + 1) * 512], AF.Identity, ["t1g", "rsg"], [("xw", 1)],
                    scale=ssg[:, 4 + g:5 + g])

        def B_out(c):
            pv = bankb(7)
            for k in range(8):
                tr(pv[:, k * 128:(k + 1) * 128], gn[:, k * 128:(k + 1) * 128], [("xw", 1)], [pk(7)])
            cp(ACT, ybt[:, c, :], pv, [pk(7)], [("yb", c)])
            if c + 1 < NT:
                dma(SP, sz_[:], z_s[(c + 1) * 128:(c + 2) * 128, :], [("z_s", c + 1)], ["sz"])

        dma(SP, sz_[:], z_s[0:128, :], [("z_s", 0)], ["sz"])
        A_builds(0); A_yoff(0); A_dve_nodep(0); A_pe2(0); A_dve_dep(0); A_decay(0)
        for c in range(NT):
            nx = c + 1 < NT
            if nx:
                A_builds(c + 1)
            B_front(c)
            if nx:
                A_yoff(c + 1)
                A_dve_nodep(c + 1)
                A_pe2(c + 1)
            B_gate(c)
            B_norm(c)
            if nx:
                A_dve_dep(c + 1)
            B_out(c)
            if nx:
                A_decay(c + 1)
        P.barrier()

        A7 = Alloc(155456)
        w13t, _ = sbt("w13t", [128, 8, 5632], BF16, 41984)
        g1_bc = A7("g1_bc", [128, D])
        xt7 = [A7("xt7_%d" % i, [128, D]) for i in range(2)]
        gmt7 = [A7("gmt7_%d" % i, [128, D], BF16) for i in range(2)]
        x1t = [A7("x1t_%d" % i, [128, D]) for i in range(2)]
        assert A7.off <= 180032, A7.off
        dma(POOL, w13t[:], w13.rearrange("(k p) n -> p k n", p=128), [], ["w13"])
        dma(SP, g1_bc[:], prow(m_s[b:b + 1, 2 * D:3 * D]), ["m_s"], ["g1_bc"])
        for k in range(16):
            if k >= 8:
                ts(DVE, wo[:, k, :], wo[:, k, :], snw_t[:, k - 8:k - 7], None, ALU.mult, None, ["wo", "snw"], [("wok", k)])
            if k < 8:
                pass
        for c in range(NT):
            t0 = c * 128
            dma(SP, xt7[c % 2][:], x[b, t0:t0 + 128, :], [], [("xt7", c % 2)])
            dma(SP, gmt7[c % 2][:], gm_s[c], [("gm_s", c)], [("gmt7", c % 2)])
            for hf in range(2):
                bk = (c % 2) * 2 + hf
                for k in range(16):
                    lhs = gmt7[c % 2][:, k * 128:(k + 1) * 128] if k < 8 else ybt[:, c, (k - 8) * 128:(k - 7) * 128]
                    mm(bank(bk), lhs, wo[:, k, hf * 512:(hf + 1) * 512], k == 0, k == 15,
                       [("gmt7", c % 2), ("yb", c), "wo" if k < 8 else ("wok", k)], [pk(bk)])
                tt(DVE, x1t[c % 2][:, hf * 512:(hf + 1) * 512], bank(bk), g1_bc[:, hf * 512:(hf + 1) * 512], ALU.mult,
                   [pk(bk), "g1_bc"], [("x1t", c % 2)])
                tt(DVE, x1t[c % 2][:, hf * 512:(hf + 1) * 512], x1t[c % 2][:, hf * 512:(hf + 1) * 512],
                   xt7[c % 2][:, hf * 512:(hf + 1) * 512], ALU.add, [("xt7", c % 2), ("x1t", c % 2)], [("x1t", c % 2)])
            dma(POOL, x1_s[t0:t0 + 128, :], x1t[c % 2][:], [("x1t", c % 2)], [("x1_s", c)])
        P.barrier()

        import os as _os
        if "8" in _os.environ.get("KSKIP", ""):
            continue
        A8 = Alloc(RA)
        xseg = A8("xseg", [128, 4, D]); xn = A8("xn", [128, 4, D], BF16)
        hl2T = A8("hl2T", [128, 8, 512], BF16)
        assert A8.off <= 41984, A8.off
        w2t, _ = sbt("w2t", [128, 22, D], BF16, 132096)
        A8 = Alloc(177152)
        hT = A8("hT", [128, 22, 512], BF16)
        sg = A8("sg", [128, 512]); fnw_bc = A8("fnw_bc", [128, D]); yo = A8("yo", [128, D])
        ss = A8("ss", [128, 16]); ss3 = A8("ss3", [128, 16])
        g2t = yo
        dma(POOL, w2t[:], w2.rearrange("(k p) n -> p k n", p=128), [], ["w2"])
        dma(SP, g2t[:], prow(m_s[b:b + 1, 5 * D:6 * D]), ["m_s"], ["g2_bc"])
        dma(SP, fnw_bc[:], prow(fnw), [], ["fnw_bc"])
        ssh = [ss, ss3]
        ss3h = [A8("ss3a", [128, 16]), A8("ss3b", [128, 16])]
        for s0 in range(0, T, 512):
            for hh in range(2):
                norm_to_T(x1_s[s0 + hh * 256:s0 + (hh + 1) * 256, :], 256, scale2[b], modT[b][:, 24:32],
                          [("scale2", b), ("modT", b)], hl2T[:, :, hh * 256:(hh + 1) * 256], "hl2T",
                          xseg[:, hh * 2:hh * 2 + 2, :], xn[:, hh * 2:hh * 2 + 2, :], None, ssh[hh], kb=hh)
            for j in range(22):
                bg, bu = 4 + (j % 2) * 2, 5 + (j % 2) * 2
                for k in range(8):
                    mm(bank(bg), w13t[:, k, j * 128:(j + 1) * 128], hl2T[:, k, :], k == 0, k == 7, ["w13", "hl2T"], [pk(bg)])
                for k in range(8):
                    mm(bank(bu), w13t[:, k, 2816 + j * 128:2816 + (j + 1) * 128], hl2T[:, k, :], k == 0, k == 7, ["w13", "hl2T"], [pk(bu)])
                act(sg[:], bank(bg), AF.Silu, [pk(bg)], ["sg"])
                tt(DVE, hT[:, j, :], sg[:], bank(bu), ALU.mult, ["sg", pk(bu)], [("hT", j)])
            if s0 == 0:
                for k in range(22):
                    tt(DVE if k % 2 == 0 else POOL, w2t[:, k, :], w2t[:, k, :], g2t[:], ALU.mult,
                       ["w2", ("w2k", k), "g2_bc"], [("w2k", k), "yo"])
            for hh in range(2):
                s3 = ss3h[hh]
                ms(DVE, s3[:, 0:4], 0.0, [("ss3", hh)])
                for ii in range(2):
                    i = hh * 2 + ii
                    xk = ("xseg", hh, ii)
                    for hf in range(2):
                        bk = (i % 2) * 2 + hf
                        for j in range(22):
                            mm(bank(bk), hT[:, j, i * 128:(i + 1) * 128], w2t[:, j, hf * 512:(hf + 1) * 512], j == 0, j == 21,
                               [("hT", j), ("w2k", j)], [pk(bk)])
                        tt(DVE, xseg[:, i, hf * 512:(hf + 1) * 512], xseg[:, i, hf * 512:(hf + 1) * 512], bank(bk), ALU.add,
                           [xk, pk(bk)], [xk])
                    act(xn[:, i, :], xseg[:, i, :], AF.Square, [xk], [("xn", hh, ii), ("ss3", hh)], accum=s3[:, ii:ii + 1])
                act(s3[:, 4:6], s3[:, 0:2], AF.Ln, [("ss3", hh), "eps1"], [("ss3", hh)], bias=eps1[:], scale=1.0 / D)
                act(s3[:, 8:10], s3[:, 4:6], AF.Exp, [("ss3", hh)], [("rs3", hh)], scale=-0.5)
                for ii in range(2):
                    i = hh * 2 + ii
                    xk = ("xseg", hh, ii)
                    stt(DVE, yo[:], xseg[:, i, :], s3[:, 8 + ii:9 + ii], fnw_bc[:], ALU.mult, ALU.mult, [xk, ("rs3", hh), "fnw_bc"], ["yo"])
                    dma(POOL, out[b, s0 + i * 128:s0 + (i + 1) * 128, :], yo[:], ["yo"], [("out", s0 // 128 + i)])
        P.barrier()

    run_prog(nc, P)
    return nc


def _prep_common(inp):
    f = np.float32
    g = lambda k: np.asarray(inp[k], dtype=f)
    fm = lambda v: np.ascontiguousarray(v.reshape(-1, 128).T)
    cm = {
        "w_ada": np.ascontiguousarray(g("w_ada")[0]), "b_ada": np.ascontiguousarray(g("b_ada")[0].reshape(1, -1)),
        "w_in": np.ascontiguousarray(g("w_in")[0]), "w_out": np.ascontiguousarray(g("w_out")[0]),
        "w13": np.ascontiguousarray(g("ffn_w13")[0]), "w2": np.ascontiguousarray(g("ffn_w2")[0]),
        "n1w": fm(g("norm1_w")[0]), "n2w": fm(g("norm2_w")[0]), "snw": fm(g("ssd_norm_w")[0]),
        "gnw": np.ascontiguousarray(g("gm_norm_w")[0].reshape(1, -1)),
        "bsv": np.ascontiguousarray(g("gm_bs")[0].reshape(1, -1)),
        "fnw": np.ascontiguousarray(g("final_norm_w").reshape(1, -1)),
        "wsT": np.ascontiguousarray(g("gm_ws")[0].transpose(2, 0, 1)),
        "cw": np.ascontiguousarray(g("conv_w")[0].reshape(5, 12, 128).transpose(2, 1, 0)),
        "cbv": fm(g("conv_b")[0]),
        "alog": np.ascontiguousarray(g("ssd_A_log")[0].reshape(1, 32)),
        "dtbias": np.ascontiguousarray(g("ssd_dt_bias")[0].reshape(1, 32)),
        "dD": np.ascontiguousarray(g("ssd_D")[0].reshape(1, 16)),
    }
    return cm


_NC_CACHE = {}


def run_cores(inp, ncores, NB):
    x = np.asarray(inp["x"], dtype=np.float32)
    ctx = np.asarray(inp["ctx"], dtype=np.float32)
    c = np.asarray(inp["c"], dtype=np.float32)
    c_ctx = np.asarray(inp["c_ctx"], dtype=np.float32)
    T, TC = x.shape[1], ctx.shape[1]
    key = (T, TC, NB)
    if key not in _NC_CACHE:
        _NC_CACHE[key] = build(T, TC, NB)
    nc = _NC_CACHE[key]
    cm = _prep_common(inp)
    maps = []
    for i in range(ncores):
        sl = slice(i * NB, (i + 1) * NB)
        vecs = np.concatenate([c[sl], c_ctx[None, :]], axis=0)
        cT = np.ascontiguousarray(vecs.reshape(NB + 1, 8, 128).transpose(2, 1, 0))
        m = dict(cm)
        m["x"] = np.ascontiguousarray(x[sl]); m["ctx"] = np.ascontiguousarray(ctx[sl]); m["cT"] = cT
        maps.append(m)
    res = run_bass_kernel_spmd(nc, maps, core_ids=list(range(ncores)))
    return np.concatenate([np.asarray(r["out"]) for r in res.results], axis=0).astype(np.float32)


def kernel(**inputs):
    return run_cores(inputs, 8, 2)
```
